# Optimizing a Trainium2 kernel written in Bass

```python
import math
import jax
import jax.numpy as jnp
from jax import lax
import numpy as np


D_MODEL = 1024
BATCH = 2
SEQ = 16384
DEPTH = 4

CTX_LEN = 256
GRID_W = 64
N_MIXERS = 4
Q_BLOCK = 128
WINDOW = 128
ROPE_THETA = 10000.0
NORM_EPS = 1e-6
WIDTH = D_MODEL

A_HEADS = 16
A_Q_LORA = 256
A_KV_LORA = 128
A_NOPE = 64
A_ROPE = 32
A_V = 64
A_IN = A_Q_LORA + A_KV_LORA + A_ROPE + WIDTH

B_HEADS = 8
B_HEAD = 64
B_IN = 3 * (2 * B_HEADS * B_HEAD) + WIDTH

C_HEADS = 8
C_KV_HEADS = 2
C_HEAD = 128
C_IN = (C_HEADS + 2 * C_KV_HEADS) * C_HEAD + WIDTH

D_HEADS = 16
D_KV_HEADS = 2
D_HEAD = 64
D_IN = (D_HEADS + 2 * D_KV_HEADS) * D_HEAD + WIDTH

DEEPNORM_ALPHA = (2 * DEPTH) ** 0.25
DEEPNORM_BETA = (8 * DEPTH) ** -0.25

kernel_name = 'hybrid_interleaved_mla_diff_gqa_swa_dit'


def rms_norm(x, g):
    xf = x.astype(jnp.float32)
    y = xf * lax.rsqrt(jnp.mean(xf * xf, axis=-1, keepdims=True) + NORM_EPS)
    return (y * g.astype(jnp.float32)).astype(x.dtype)


def layer_norm(x, g, b):
    xf = x.astype(jnp.float32)
    xc = xf - jnp.mean(xf, axis=-1, keepdims=True)
    var = jnp.mean(xc * xc, axis=-1, keepdims=True)
    y = xc * lax.rsqrt(var + NORM_EPS) * g.astype(jnp.float32) + b.astype(jnp.float32)
    return y.astype(x.dtype)


def axial_rope_tables(rows, rot_dim):
    row = jnp.repeat(jnp.arange(rows, dtype=jnp.float32), GRID_W)
    col = jnp.tile(jnp.arange(GRID_W, dtype=jnp.float32), rows)
    n_freq = rot_dim // 4
    inv_freq = ROPE_THETA ** (-jnp.arange(n_freq, dtype=jnp.float32) / n_freq)
    ang = jnp.concatenate([row[:, None] * inv_freq, col[:, None] * inv_freq], axis=-1)
    return jnp.cos(ang)[:, None, :], jnp.sin(ang)[:, None, :]


def apply_rope(x, cos, sin):
    half = x.shape[-1] // 2
    x1 = x[..., :half].astype(jnp.float32)
    x2 = x[..., half:].astype(jnp.float32)
    return jnp.concatenate([x1 * cos - x2 * sin, x2 * cos + x1 * sin], axis=-1).astype(x.dtype)


def adaln(cvec, w, b):
    m = jax.nn.silu(cvec) @ w + b
    return jnp.split(m, 3, axis=-1)


def gated(o, gate):
    return o.reshape(gate.shape) * jax.nn.silu(gate)


def dense_attention(q, k, v, scale):
    bsz, s_len, hk, g, d = q.shape
    dv = v.shape[-1]
    nb = s_len // Q_BLOCK
    q_blocks = jnp.moveaxis(q.reshape(bsz, nb, Q_BLOCK, hk, g, d), 1, 0)

    def one_block(qb):
        s = jnp.einsum('bqhgd,bkhd->bhgqk', qb, k).astype(jnp.float32) * scale
        p = jax.nn.softmax(s, axis=-1).astype(v.dtype)
        return jnp.einsum('bhgqk,bkhv->bqhgv', p, v)

    o = lax.map(one_block, q_blocks)
    return jnp.moveaxis(o, 0, 1).reshape(bsz, s_len, hk, g, dv)


def prefix_attention(q, k, v, q_c, k_c, v_c, scale, need_ctx):
    o = dense_attention(q, jnp.concatenate([k, k_c], axis=1), jnp.concatenate([v, v_c], axis=1), scale)
    o_c = dense_attention(q_c, k_c, v_c, scale) if need_ctx else None
    return o, o_c


def window_sink_attention(q, k, v, k_c, v_c, sinks, scale):
    bsz, s_len, hk, g, d = q.shape
    dv = v.shape[-1]
    nb = s_len // Q_BLOCK
    kw = 3 * Q_BLOCK
    pad = ((0, 0), (Q_BLOCK, Q_BLOCK), (0, 0), (0, 0))
    k_pad = jnp.pad(k, pad)
    v_pad = jnp.pad(v, pad)
    offs_q = jnp.arange(Q_BLOCK)
    offs_k = jnp.arange(kw) - Q_BLOCK
    in_band = jnp.abs(offs_k[None, :] - offs_q[:, None]) <= WINDOW
    sink_logit = sinks.reshape(hk, g).astype(jnp.float32)[None, :, :, None, None]

    def one_block(n):
        start = n * Q_BLOCK
        qb = lax.dynamic_slice_in_dim(q, start, Q_BLOCK, axis=1)
        kb = lax.dynamic_slice_in_dim(k_pad, start, kw, axis=1)
        vb = lax.dynamic_slice_in_dim(v_pad, start, kw, axis=1)
        kpos = start + offs_k
        valid = in_band & ((kpos >= 0) & (kpos < s_len))[None, :]
        s_loc = jnp.einsum('bqhgd,bkhd->bhgqk', qb, kb).astype(jnp.float32) * scale
        s_loc = jnp.where(valid, s_loc, -jnp.inf)
        s_ctx = jnp.einsum('bqhgd,bkhd->bhgqk', qb, k_c).astype(jnp.float32) * scale
        s_sink = jnp.broadcast_to(sink_logit, s_loc.shape[:-1] + (1,))
        p = jax.nn.softmax(jnp.concatenate([s_loc, s_ctx, s_sink], axis=-1), axis=-1).astype(v.dtype)
        o_loc = jnp.einsum('bhgqk,bkhv->bqhgv', p[..., :kw], vb)
        o_ctx = jnp.einsum('bhgqk,bkhv->bqhgv', p[..., kw:kw + k_c.shape[1]], v_c)
        return o_loc + o_ctx

    o = lax.map(one_block, jnp.arange(nb))
    return jnp.moveaxis(o, 0, 1).reshape(bsz, s_len, hk, g, dv)


def context_sink_attention(q_c, k_c, v_c, sinks, scale):
    hk, g = q_c.shape[2], q_c.shape[3]
    s = jnp.einsum('bqhgd,bkhd->bhgqk', q_c, k_c).astype(jnp.float32) * scale
    s_sink = jnp.broadcast_to(sinks.reshape(hk, g).astype(jnp.float32)[None, :, :, None, None], s.shape[:-1] + (1,))
    p = jax.nn.softmax(jnp.concatenate([s, s_sink], axis=-1), axis=-1)[..., :-1].astype(v_c.dtype)
    return jnp.einsum('bhgqk,bkhv->bqhgv', p, v_c)


def mla_mixer(h, h_c, rows, need_ctx, w_in, g_qa, w_qb, g_kva, w_kvb):
    cos, sin = axial_rope_tables(rows, A_ROPE)
    splits = [A_Q_LORA, A_Q_LORA + A_KV_LORA, A_Q_LORA + A_KV_LORA + A_ROPE]

    def project(t, rope):
        bsz, n = t.shape[:2]
        q_lat, kv_lat, k_pe, gate = jnp.split(t @ w_in, splits, axis=-1)
        q = (rms_norm(q_lat, g_qa) @ w_qb).reshape(bsz, n, A_HEADS, A_NOPE + A_ROPE)
        kv = (rms_norm(kv_lat, g_kva) @ w_kvb).reshape(bsz, n, A_HEADS, A_NOPE + A_V)
        q_nope, q_pe = q[..., :A_NOPE], q[..., A_NOPE:]
        k_nope, v = kv[..., :A_NOPE], kv[..., A_NOPE:]
        k_pe = k_pe[:, :, None, :]
        if rope:
            q_pe = apply_rope(q_pe, cos, sin)
            k_pe = apply_rope(k_pe, cos, sin)
        q = jnp.concatenate([q_nope, q_pe], axis=-1)[:, :, :, None, :]
        k = jnp.concatenate([k_nope, jnp.broadcast_to(k_pe, (bsz, n, A_HEADS, A_ROPE))], axis=-1)
        return q, k, v, gate

    q, k, v, gate = project(h, True)
    q_c, k_c, v_c, gate_c = project(h_c, False)
    o, o_c = prefix_attention(q, k, v, q_c, k_c, v_c, (A_NOPE + A_ROPE) ** -0.5, need_ctx)
    return gated(o, gate), (gated(o_c, gate_c) if need_ctx else None)


def diff_mixer(h, h_c, rows, need_ctx, layer_idx, w_in, lam, g_sub):
    cos, sin = axial_rope_tables(rows, B_HEAD)
    lambda_init = 0.8 - 0.6 * math.exp(-0.3 * layer_idx)
    lam = lam.astype(jnp.float32)
    lam_full = jnp.exp(jnp.sum(lam[0] * lam[1])) - jnp.exp(jnp.sum(lam[2] * lam[3])) + lambda_init
    scale = B_HEAD ** -0.5

    def project(t, rope):
        bsz, n = t.shape[:2]
        q, k, v, gate = jnp.split(t @ w_in, 4, axis=-1)
        q = q.reshape(bsz, n, 2 * B_HEADS, B_HEAD)
        k = k.reshape(bsz, n, 2 * B_HEADS, B_HEAD)
        if rope:
            q = apply_rope(q, cos, sin)
            k = apply_rope(k, cos, sin)
        q = q.reshape(bsz, n, B_HEADS, 2, B_HEAD)
        k = k.reshape(bsz, n, B_HEADS, 2, B_HEAD)
        v = v.reshape(bsz, n, B_HEADS, 2 * B_HEAD)
        return q, k, v, gate

    def combine(o1, o2, gate):
        o = o1 - lam_full.astype(o1.dtype) * o2
        o = rms_norm(o, g_sub) * (1.0 - lambda_init)
        return gated(o, gate)

    q, k, v, gate = project(h, True)
    q_c, k_c, v_c, gate_c = project(h_c, False)
    o1, o1_c = prefix_attention(q[:, :, :, 0:1], k[:, :, :, 0], v, q_c[:, :, :, 0:1], k_c[:, :, :, 0], v_c, scale, need_ctx)
    o2, o2_c = prefix_attention(q[:, :, :, 1:2], k[:, :, :, 1], v, q_c[:, :, :, 1:2], k_c[:, :, :, 1], v_c, scale, need_ctx)
    return combine(o1, o2, gate), (combine(o1_c, o2_c, gate_c) if need_ctx else None)


def qknorm_gqa_mixer(h, h_c, rows, need_ctx, w_in, g_q, g_k):
    cos, sin = axial_rope_tables(rows, C_HEAD)
    splits = [C_HEADS * C_HEAD, (C_HEADS + C_KV_HEADS) * C_HEAD, (C_HEADS + 2 * C_KV_HEADS) * C_HEAD]

    def project(t, rope):
        bsz, n = t.shape[:2]
        q, k, v, gate = jnp.split(t @ w_in, splits, axis=-1)
        q = rms_norm(q.reshape(bsz, n, C_HEADS, C_HEAD), g_q)
        k = rms_norm(k.reshape(bsz, n, C_KV_HEADS, C_HEAD), g_k)
        v = v.reshape(bsz, n, C_KV_HEADS, C_HEAD)
        if rope:
            q = apply_rope(q, cos, sin)
            k = apply_rope(k, cos, sin)
        q = q.reshape(bsz, n, C_KV_HEADS, C_HEADS // C_KV_HEADS, C_HEAD)
        return q, k, v, gate

    q, k, v, gate = project(h, True)
    q_c, k_c, v_c, gate_c = project(h_c, False)
    o, o_c = prefix_attention(q, k, v, q_c, k_c, v_c, C_HEAD ** -0.5, need_ctx)
    return gated(o, gate), (gated(o_c, gate_c) if need_ctx else None)


def swa_sink_mixer(h, h_c, rows, need_ctx, w_in, sinks):
    cos, sin = axial_rope_tables(rows, D_HEAD)
    splits = [D_HEADS * D_HEAD, (D_HEADS + D_KV_HEADS) * D_HEAD, (D_HEADS + 2 * D_KV_HEADS) * D_HEAD]
    scale = D_HEAD ** -0.5

    def project(t, rope):
        bsz, n = t.shape[:2]
        q, k, v, gate = jnp.split(t @ w_in, splits, axis=-1)
        q = q.reshape(bsz, n, D_HEADS, D_HEAD)
        k = k.reshape(bsz, n, D_KV_HEADS, D_HEAD)
        v = v.reshape(bsz, n, D_KV_HEADS, D_HEAD)
        if rope:
            q = apply_rope(q, cos, sin)
            k = apply_rope(k, cos, sin)
        q = q.reshape(bsz, n, D_KV_HEADS, D_HEADS // D_KV_HEADS, D_HEAD)
        return q, k, v, gate

    q, k, v, gate = project(h, True)
    q_c, k_c, v_c, gate_c = project(h_c, False)
    o = window_sink_attention(q, k, v, k_c, v_c, sinks, scale)
    o_c = gated(context_sink_attention(q_c, k_c, v_c, sinks, scale), gate_c) if need_ctx else None
    return gated(o, gate), o_c


def setup_inputs(seed: int = 0) -> dict:
    key = jax.random.key(seed)
    ks = iter(jax.random.split(key, 32))

    def nrm(shape, std):
        return jax.random.normal(next(ks), shape, jnp.float32) * std

    def gain(shape):
        return 1.0 + nrm(shape, 0.02)

    n_a = len(range(0, DEPTH, N_MIXERS))
    n_b = len(range(1, DEPTH, N_MIXERS))
    n_c = len(range(2, DEPTH, N_MIXERS))
    n_d = len(range(3, DEPTH, N_MIXERS))
    return {
        'x': nrm((BATCH, SEQ, D_MODEL), 1.0),
        'c': nrm((BATCH, D_MODEL), 1.0),
        'ctx': nrm((BATCH, CTX_LEN, D_MODEL), 1.0),
        'c_ctx': nrm((D_MODEL,), 1.0),
        'ada_w': nrm((DEPTH, D_MODEL, 3 * D_MODEL), 0.02),
        'ada_b': nrm((DEPTH, 3 * D_MODEL), 0.02),
        'out_w': nrm((DEPTH, WIDTH, D_MODEL), DEEPNORM_BETA * WIDTH ** -0.5),
        'ln_g': gain((DEPTH, D_MODEL)),
        'ln_b': nrm((DEPTH, D_MODEL), 0.02),
        'mla_w_in': nrm((n_a, D_MODEL, A_IN), D_MODEL ** -0.5),
        'mla_g_qa': gain((n_a, A_Q_LORA)),
        'mla_w_qb': nrm((n_a, A_Q_LORA, A_HEADS * (A_NOPE + A_ROPE)), A_Q_LORA ** -0.5),
        'mla_g_kva': gain((n_a, A_KV_LORA)),
        'mla_w_kvb': nrm((n_a, A_KV_LORA, A_HEADS * (A_NOPE + A_V)), A_KV_LORA ** -0.5),
        'diff_w_in': nrm((n_b, D_MODEL, B_IN), D_MODEL ** -0.5),
        'diff_lambda': nrm((n_b, 4, B_HEAD), 0.1),
        'diff_g_sub': gain((n_b, 2 * B_HEAD)),
        'gqa_w_in': nrm((n_c, D_MODEL, C_IN), D_MODEL ** -0.5),
        'gqa_g_q': gain((n_c, C_HEAD)),
        'gqa_g_k': gain((n_c, C_HEAD)),
        'swa_w_in': nrm((n_d, D_MODEL, D_IN), D_MODEL ** -0.5),
        'swa_sink': nrm((n_d, D_HEADS), 0.5),
    }


def reference(x, c, ctx, c_ctx, ada_w, ada_b, out_w, ln_g, ln_b,
              mla_w_in, mla_g_qa, mla_w_qb, mla_g_kva, mla_w_kvb,
              diff_w_in, diff_lambda, diff_g_sub,
              gqa_w_in, gqa_g_q, gqa_g_k,
              swa_w_in, swa_sink):
    ROWS = x.shape[1] // GRID_W
    for i in range(DEPTH):
        kind, j = i % N_MIXERS, i // N_MIXERS
        need_ctx = i < DEPTH - 1
        shift, scale, gate = adaln(c, ada_w[i], ada_b[i])
        shift_c, scale_c, gate_c = adaln(c_ctx, ada_w[i], ada_b[i])
        h = x * (1.0 + scale[:, None, :]) + shift[:, None, :]
        h_c = ctx * (1.0 + scale_c) + shift_c
        if kind == 0:
            o, o_c = mla_mixer(h, h_c, ROWS, need_ctx, mla_w_in[j], mla_g_qa[j], mla_w_qb[j], mla_g_kva[j], mla_w_kvb[j])
        elif kind == 1:
            o, o_c = diff_mixer(h, h_c, ROWS, need_ctx, i, diff_w_in[j], diff_lambda[j], diff_g_sub[j])
        elif kind == 2:
            o, o_c = qknorm_gqa_mixer(h, h_c, ROWS, need_ctx, gqa_w_in[j], gqa_g_q[j], gqa_g_k[j])
        else:
            o, o_c = swa_sink_mixer(h, h_c, ROWS, need_ctx, swa_w_in[j], swa_sink[j])
        x = layer_norm(DEEPNORM_ALPHA * x + gate[:, None, :] * (o @ out_w[i]), ln_g[i], ln_b[i])
        if need_ctx:
            ctx = layer_norm(DEEPNORM_ALPHA * ctx + gate_c * (o_c @ out_w[i]), ln_g[i], ln_b[i])
    return x
```

```python
import math
import types
import numpy as np
import ml_dtypes
import concourse.bass as bass
import concourse.mybir as mybir
from concourse.bass_utils import run_bass_kernel_spmd

F32 = mybir.dt.float32
BF16 = mybir.dt.bfloat16
AF = mybir.ActivationFunctionType
ALU = mybir.AluOpType

ENGS = ("tensor", "scalar", "vector", "gpsimd", "sync")

D = 1024
SEQ = 16384
CTX = 256
NCORE = 8
TOK = 4096
NQ = TOK + CTX
NKF = SEQ + CTX
NK3 = TOK + 8 * 128 + CTX
EPS = 1e-6
ALPHA = (2 * 4) ** 0.25
GRID_W = 64
THETA = 10000.0


class T:
    __slots__ = ("h", "name", "w", "r", "dsem", "dcnt", "keep")

    def __init__(self, h, name):
        self.h = h
        self.name = name
        self.w = {}
        self.r = {}
        self.dsem = None
        self.dcnt = 0
        self.keep = False

    def __getitem__(self, k):
        return self.h[k]

    def ap(self):
        return self.h.ap()


class Ins:
    __slots__ = ("eng", "fn", "waits", "idx", "inc", "val", "dsem", "dinc")

    def __init__(self, eng, fn):
        self.eng = eng
        self.fn = fn
        self.waits = []
        self.idx = 0
        self.inc = False
        self.val = 0
        self.dsem = None


def freeze(fn):
    if fn.__closure__ is None:
        return fn
    cells = []
    for c in fn.__closure__:
        try:
            cells.append(types.CellType(c.cell_contents))
        except ValueError:
            cells.append(c)
    return types.FunctionType(fn.__code__, fn.__globals__, fn.__name__, fn.__defaults__, tuple(cells))


class Prog:
    def __init__(self, nc):
        self.nc = nc
        self.q = {e: [] for e in ENGS}
        self.waited = {e: {} for e in ENGS}
        self.esem = {}
        self.dsems = []
        self.semobj = {}
        self.off = 16512
        self.uid = 0
        self.sem_pool = []
        self.scope = []

    def sb(self, name, shape, dt):
        nbytes = int(np.prod(shape[1:])) * (2 if dt == BF16 else 4)
        nbytes = (nbytes + 63) // 64 * 64
        self.uid += 1
        h = self.nc.alloc_sbuf_tensor_at(f"{name}_{self.uid}", list(shape), dt, offset=self.off)
        self.off += nbytes
        assert self.off <= 229344, ("SBUF overflow", name, self.off)
        t = T(h, name)
        self.scope.append(t)
        return t

    def ps(self, name, shape, dt=F32):
        return T(self.nc.alloc_psum_tensor(name, list(shape), dt), name)

    def dram(self, name, shape, dt, kind="Internal"):
        return T(self.nc.dram_tensor(name, list(shape), dt, kind=kind), name)

    def op(self, eng, fn, reads=(), writes=(), dma=None, acc=False, dma_inc=16):
        ins = Ins(eng, freeze(fn))
        lst = self.q[eng]
        ins.idx = len(lst)
        deps = {}

        def add(d, skipkey=None):
            for k, vo in d.items():
                if k == skipkey:
                    continue
                if k not in deps or deps[k][0] < vo[0]:
                    deps[k] = vo

        if dma is not None:
            if dma.dsem is None:
                if self.sem_pool:
                    dma.dsem, dma.dcnt = self.sem_pool.pop()
                else:
                    self.nsem = getattr(self, "nsem", 0) + 1
                    dma.dsem = self.nc.alloc_semaphore("d_%s_%d" % (dma.name, self.nsem))
                self.dsems.append(dma)
            mykey = ("d", id(dma))
            self.semobj[mykey] = dma
        else:
            mykey = eng
        for t in reads:
            add(t.w)
        for t in writes:
            add(t.w, mykey if acc else None)
            add(t.r)
        waited = self.waited[eng]
        for k, (v, o) in deps.items():
            if k == "tensor" and eng == "tensor":
                continue
            if waited.get(k, -1) >= v:
                continue
            waited[k] = v
            ins.waits.append((k, v, o, None if o is not None else self.semobj[k].dsem))
            if o is not None:
                o.inc = True
        if dma is not None:
            dma.dcnt += dma_inc
            ins.dsem = dma.dsem
            ins.dinc = dma_inc
            comp = (dma.dcnt, None)
        else:
            comp = (ins.idx, ins)
        for t in reads:
            cur = t.r.get(mykey)
            if cur is None or cur[0] < comp[0]:
                t.r[mykey] = comp
        for t in writes:
            if acc:
                t.w[mykey] = comp
            else:
                t.w = {mykey: comp}
            t.r = {}
        lst.append(ins)
        return ins

    def barrier(self, release=True):
        marks = []
        for e in ENGS:
            last = None
            for ins in reversed(self.q[e]):
                if ins.dsem is None and not getattr(ins.fn, "_isnop", False):
                    last = ins
                    break
            if last is not None:
                m = T(None, "bar_" + e)
                m.w = {e: (last.idx, last)}
                marks.append(m)
        sm = T(None, "bar_sync")
        nop1 = lambda eng: eng.nop()
        ins = self.op("sync", nop1, reads=marks, writes=[sm])
        ins.fn._isnop = True
        waited = self.waited["sync"]
        for t in self.dsems:
            k = ("d", id(t))
            if waited.get(k, -1) < t.dcnt:
                waited[k] = t.dcnt
                ins.waits.append((k, t.dcnt, None, t.dsem))
        for e in ENGS:
            if e != "sync":
                i2 = self.op(e, lambda eng: eng.nop(), reads=marks + [sm])
                i2.fn._isnop = True
        for e in ENGS:
            for t in self.dsems:
                k = ("d", id(t))
                if self.waited[e].get(k, -1) < t.dcnt:
                    self.waited[e][k] = t.dcnt
        if release:
            for t in self.scope:
                if t.dsem is not None and not getattr(t, "keep", False):
                    self.sem_pool.append((t.dsem, t.dcnt))
                    self.dsems.remove(t)
                    t.dsem = None
            self.scope = [t for t in self.scope if getattr(t, "keep", False)]

    def emit(self):
        nc = self.nc
        for e in ENGS:
            c = 0
            for ins in self.q[e]:
                if ins.inc:
                    c += 1
                ins.val = c
        for e in ENGS:
            self.esem[e] = nc.alloc_semaphore("e_" + e)

        def run(engname, eng):
            for ins in self.q[engname]:
                for (k, v, o, hnd) in ins.waits:
                    if o is not None:
                        eng.wait_ge(self.esem[k], o.val)
                    else:
                        eng.wait_ge(hnd, v)
                r = ins.fn(eng)
                if ins.dsem is not None:
                    r.then_inc(ins.dsem, ins.dinc)
                elif ins.inc:
                    r.then_inc(self.esem[engname], 1)

        with nc.Block() as block:
            @block.tensor
            def _(eng):
                run("tensor", eng)

            @block.scalar
            def _(eng):
                run("scalar", eng)

            @block.vector
            def _(eng):
                run("vector", eng)

            @block.gpsimd
            def _(eng):
                run("gpsimd", eng)

            @block.sync
            def _(eng):
                run("sync", eng)


def layer_cfg(L):
    c = {}
    if L == 0:
        c.update(nk=NKF, nq=NQ, scale=96 ** -0.5, vcols=1024, qrows=16 * 96, krows=1024, fin="dv64")
        c["groups"] = [dict(kparts=[("KN", h * 64, 64, 0), ("KPE", 0, 32, 64)], vc0=h * 64, dv=64,
                            qtiles=[dict(q0=h * 96, rows=96, maps=[dict(pb=0, dq=96, o0=h * 64)])])
                       for h in range(16)]
    elif L == 1:
        c.update(nk=NKF, nq=NQ, scale=64 ** -0.5, vcols=1024, qrows=1024, krows=1024, fin="diff")
        c["groups"] = [dict(kparts=[("KT", h * 128, 128, 0)], vc0=h * 128, dv=128,
                            qtiles=[dict(loads=[(0, h * 128, 64, 0), (1, h * 128 + 64, 64, 64)],
                                         maps=[dict(qi=0, pb=0, dq=128, o0=h * 128), dict(qi=1, pb=0, dq=128, o0=h * 128)])])
                       for h in range(8)]
    elif L == 2:
        c.update(nk=NKF, nq=NQ, scale=128 ** -0.5, vcols=256, qrows=1024, krows=256, fin="dv128")
        c["groups"] = [dict(kparts=[("KT", g * 128, 128, 0)], vc0=g * 128, dv=128,
                            qtiles=[dict(q0=hq * 128, rows=128, maps=[dict(pb=0, dq=128, o0=hq * 128)])
                                    for hq in range(4 * g, 4 * g + 4)])
                       for g in range(2)]
    else:
        c.update(nk=NK3, nq=TOK, scale=64 ** -0.5, vcols=128, qrows=1024, krows=128, fin="dv64")
        c["groups"] = [dict(kparts=[("KT", g * 64, 64, 0), ("KT", g * 64, 64, 64)], vc0=g * 64, dv=64,
                            qtiles=[dict(q0=t * 128, rows=128,
                                         maps=[dict(pb=0, dq=64, o0=(2 * t) * 64, head=2 * t),
                                               dict(pb=64, dq=64, o0=(2 * t + 1) * 64, head=2 * t + 1)])
                                    for t in range(4 * g, 4 * g + 4)])
                       for g in range(2)]
    return c


def chunks(n):
    out = []
    r = 0
    while r < n:
        m = min(512, n - r)
        out.append((r, m))
        r += m
    return out


def emit_layer(nc, P, L, sh):
    cfg = layer_cfg(L)
    NK, NQL = cfg["nk"], cfg["nq"]
    need_ctx = L < 3
    op = P.op

    def dram_in(name, shape, dt=F32):
        return P.dram("l%d_%s" % (L, name), shape, dt, kind="ExternalInput")

    XO, CX, GX = sh["XO"], sh["CX"], sh["GX"]
    if L == 0:
        xs = dram_in("xs", [NK, D])
        xq = dram_in("xq", [NQ, D])

    def ksrc(r0, n):
        if L == 0:
            return [(0, n // 128, xs, r0)]
        def grow(t):
            jr, w_ = t // TOK, t % TOK
            return ((w_ // 256) * 4 + jr) * 256 + (w_ % 256)
        if L in (1, 2):
            if r0 < SEQ:
                return [(2 * i, 2, GX[L - 1], grow(r0 + 256 * i)) for i in range(n // 256)]
            return [(0, n // 128, CX[L - 1], r0 - SEQ)]
        if r0 < TOK:
            return [(0, n // 128, XO[2], r0)]
        if r0 < TOK + 1024:
            res = []
            for j in range(n // 128):
                e_ = (r0 - TOK) // 128 + j
                res.append((j, 1, GX[2], grow((e_ // 2) * TOK + (0 if e_ % 2 == 0 else TOK - 128))))
            return res
        return [(0, n // 128, CX[2], r0 - TOK - 1024)]

    def qsrc(r0, n):
        if L == 0:
            return [(0, n // 128, xq, r0)]
        if r0 < TOK:
            return [(0, n // 128, XO[L - 1], r0)]
        return [(0, n // 128, CX[L - 1], r0 - TOK)]

    def odst(row):
        if L == 3:
            return sh["out"], row
        if row < TOK:
            return XO[L], row
        return CX[L], row - TOK
    cvec = dram_in("cvec", [128, 8, 2])
    ada_w = dram_in("ada_w", [D, 3 * D])
    ada_bf = dram_in("ada_bf", [128, 16])
    ada_bg = dram_in("ada_bg", [1, D])
    out_w = dram_in("out_w", [D, D])
    ln_g = dram_in("ln_g", [1, D])
    ln_b = dram_in("ln_b", [1, D])
    ident_d = dram_in("ident", [128, 128])
    tkc = dram_in("tkc", [128, NK])
    tks = dram_in("tks", [128, NK])
    tqc = dram_in("tqc", [128, NQ])
    tqs = dram_in("tqs", [128, NQ])
    if L == 0:
        wA = dram_in("wA", [D, 256 + 128 + 32 + 32])
        wG = dram_in("wG", [D, D])
        wqb = dram_in("wqb", [256, 2 * 1536])
        wkn = dram_in("wkn", [128, 1024])
        wv = dram_in("wv", [128, 1024])
        gq = dram_in("gq", [128, 2])
        gkv = dram_in("gkv", [128, 1])
    elif L == 1:
        wK = dram_in("wK", [D, 3072])
        wQ = dram_in("wQ", [D, 3072])
        lam = dram_in("lam", [1, 256])
        gsub = dram_in("gsub", [128, 1])
    elif L == 2:
        wK = dram_in("wK", [D, 768])
        wQ = dram_in("wQ", [D, 3072])
        gqk = dram_in("gqk", [128, 4])
    else:
        wK = dram_in("wK", [D, 384])
        wQ = dram_in("wQ", [D, 3072])
        sink = dram_in("sink", [1, 16])
        wmask = dram_in("wmask", [128, 14, 512])

    if L == 0:
        KN = P.dram("l%d_KN" % L, [1024, NK], BF16)
        KPE = P.dram("l%d_KPE" % L, [32, NK], BF16)
        kd = {"KN": KN, "KPE": KPE}
    else:
        KT = P.dram("l%d_KT" % L, [cfg["krows"], NK], BF16)
        kd = {"KT": KT}
    Vd = P.dram("l%d_Vd" % L, [NK, cfg["vcols"]], BF16)
    QT = P.dram("l%d_QT" % L, [cfg["qrows"], NQ], BF16)
    GT = P.dram("l%d_GT" % L, [D, NQ], BF16)
    OT = P.dram("l%d_OT" % L, [D, NQ], BF16)

    pb = sh["pb"]
    P.off = sh["sb0"]

    ident = P.sb("ident", [128, 128], F32)
    identb = P.sb("identb", [128, 128], BF16)
    onesb = P.sb("onesb", [128, 128], BF16)
    onesf = P.sb("onesf", [128, 128], F32)
    modf = P.sb("modf", [128, 16, 2], F32)
    s1p = P.sb("s1p", [128, 8, 2], F32)
    gate_bc = [P.sb("gate_bc%d" % j, [128, D], F32) for j in range(2)]
    small = P.sb("small", [128, 64], F32)
    epsc = P.sb("epsc", [128, 1], F32)
    persist_end = P.off

    def dma(eng, out_t, out_ap, in_t, in_ap, acc=False):
        return op(eng, lambda e: e.dma_start(out=out_ap, in_=in_ap), reads=[in_t], writes=[out_t], dma=out_t, acc=acc)

    dma("sync", ident, ident[:], ident_d, ident_d[:])
    op("vector", lambda e: e.tensor_copy(out=identb[:], in_=ident[:]), reads=[ident], writes=[identb])
    op("vector", lambda e: e.memset(onesb[:], 1.0), writes=[onesb])
    op("vector", lambda e: e.memset(onesf[:], 1.0), writes=[onesf])
    op("vector", lambda e: e.memset(epsc[:], EPS), writes=[epsc])

    cv = P.sb("cv", [128, 8, 2], F32)
    scv = P.sb("scv", [128, 8, 2], F32)
    scbc = P.sb("scbc", [128, 8, 2, 128], F32)
    abf = P.sb("abf", [128, 16], F32)
    abg = P.sb("abg", [128, D], F32)
    aw = [P.sb("aw%d" % i, [128, 8, 512], F32) for i in range(2)]
    dma("sync", cv, cv[:], cvec, cvec[:])
    dma("sync", abf, abf[:], ada_bf, ada_bf[:])
    dma("sync", abg, abg[:], ada_bg, ada_bg.ap()[0, :].partition_broadcast(128))
    op("scalar", lambda e: e.activation(out=scv[:], in_=cv[:], func=AF.Silu), reads=[cv], writes=[scv])
    for k in range(8):
        for j in range(2):
            op("vector", lambda e, k=k, j=j: e.tensor_scalar(out=scbc[:, k, j, :], in0=onesf[:], scalar1=scv[:, k, j:j + 1],
                                                           scalar2=None, op0=ALU.mult),
               reads=[onesf, scv], writes=[scbc], acc=True)
    aw_v = ada_w.ap().rearrange("(k p) n -> p k n", p=128)
    for pc in range(6):
        a = aw[pc % 2]
        dma("sync", a, a[:], ada_w, aw_v[:, :, pc * 512:(pc + 1) * 512])
        if pc < 4:
            for m in range(4):
                t = pc * 4 + m
                for k in range(8):
                    op("tensor", lambda e, a=a, m=m, k=k, t=t: e.matmul(pb[0][:, t * 2:t * 2 + 2], lhsT=a[:, k, m * 128:(m + 1) * 128],
                                                                     rhs=scv[:, k, :], start=(k == 0), stop=(k == 7)),
                       reads=[a, scv], writes=[pb[0]], acc=True)
        else:
            for j in range(2):
                for k in range(8):
                    op("tensor", lambda e, a=a, j=j, k=k: e.matmul(pb[1 + j][:, :], lhsT=scbc[:, k, j, :], rhs=a[:, k, :],
                                                                 start=(k == 0), stop=(k == 7)),
                       reads=[a, scbc], writes=[pb[1 + j]], acc=True)
                c0 = (pc - 4) * 512
                op("vector", lambda e, j=j, c0=c0: e.tensor_tensor(out=gate_bc[j][:, c0:c0 + 512], in0=pb[1 + j][:, :],
                                                                 in1=abg[:, c0:c0 + 512], op=ALU.add),
                   reads=[pb[1 + j], abg], writes=[gate_bc[j]], acc=True)
    for j in range(2):
        op("vector", lambda e, j=j: e.tensor_tensor(out=modf[:, :, j], in0=pb[0][:, j:32:2], in1=abf[:], op=ALU.add),
           reads=[pb[0], abf], writes=[modf], acc=True)
    op("vector", lambda e: e.tensor_scalar(out=s1p[:], in0=modf[:, 8:16, :], scalar1=1.0, scalar2=None, op0=ALU.add),
       reads=[modf], writes=[s1p])

    if L == 1:
        lam_init = 0.8 - 0.6 * math.exp(-0.3 * 1)
        lm = P.sb("lm", [1, 256], F32)
        lt = P.sb("lt", [1, 8], F32)
        gs = P.sb("gs", [128, 1], F32)
        dma("sync", lm, lm[:], lam, lam[:])
        dma("sync", gs, gs[:], gsub, gsub[:])
        op("vector", lambda e: e.tensor_tensor(out=lm[:, 0:64], in0=lm[:, 0:64], in1=lm[:, 64:128], op=ALU.mult), reads=[lm], writes=[lm])
        op("vector", lambda e: e.tensor_tensor(out=lm[:, 128:192], in0=lm[:, 128:192], in1=lm[:, 192:256], op=ALU.mult), reads=[lm], writes=[lm])
        op("vector", lambda e: e.tensor_reduce(out=lt[:, 0:1], in_=lm[:, 0:64], axis=mybir.AxisListType.X, op=ALU.add), reads=[lm], writes=[lt])
        op("vector", lambda e: e.tensor_reduce(out=lt[:, 1:2], in_=lm[:, 128:192], axis=mybir.AxisListType.X, op=ALU.add), reads=[lm], writes=[lt])
        op("scalar", lambda e: e.activation(out=lt[:, 2:4], in_=lt[:, 0:2], func=AF.Exp), reads=[lt], writes=[lt])
        op("vector", lambda e: e.tensor_tensor(out=lt[:, 4:5], in0=lt[:, 3:4], in1=lt[:, 2:3], op=ALU.subtract), reads=[lt], writes=[lt])
        op("vector", lambda e: e.tensor_scalar(out=lt[:, 4:5], in0=lt[:, 4:5], scalar1=-lam_init, scalar2=None, op0=ALU.add), reads=[lt], writes=[lt])
        op("tensor", lambda e: e.matmul(pb[3][:, 0:1], lhsT=onesf[0:1, :], rhs=lt[0:1, 4:5], start=True, stop=True),
           reads=[onesf, lt], writes=[pb[3]])
        op("vector", lambda e: e.tensor_copy(out=small[:, 0:1], in_=pb[3][:, 0:1]), reads=[pb[3]], writes=[small], acc=True)
        op("vector", lambda e: e.tensor_scalar(out=small[:, 1:2], in0=gs[:], scalar1=1.0 - lam_init, scalar2=None, op0=ALU.mult),
           reads=[gs], writes=[small], acc=True)
    if L == 3:
        sk = P.sb("sk", [128, 16], F32)
        dma("sync", sk, sk[:], sink, sink.ap()[0, :].partition_broadcast(128))
        op("scalar", lambda e: e.activation(out=small[:, 0:16], in_=sk[:], func=AF.Exp), reads=[sk], writes=[small], acc=True)
    if L == 0:
        gqs = P.sb("gqs", [128, 2], F32)
        gkvs = P.sb("gkvs", [128, 1], F32)
        dma("sync", gqs, gqs[:], gq, gq[:])
        dma("sync", gkvs, gkvs[:], gkv, gkv[:])
        op("vector", lambda e: e.tensor_copy(out=small[:, 0:2], in_=gqs[:]), reads=[gqs], writes=[small], acc=True)
        op("vector", lambda e: e.tensor_copy(out=small[:, 2:3], in_=gkvs[:]), reads=[gkvs], writes=[small], acc=True)
    if L == 2:
        gg = P.sb("gg", [128, 4], F32)
        dma("sync", gg, gg[:], gqk, gqk[:])
        op("vector", lambda e: e.tensor_copy(out=small[:, 0:4], in_=gg[:]), reads=[gg], writes=[small], acc=True)

    P.barrier()

    rr = {"cast": 0}

    def load_weights(dst, src, ncols, nk, stg):
        v = src.ap().rearrange("(k p) n -> p k n", p=128)
        step = 2048 // nk
        i = 0
        for c0 in range(0, ncols, step):
            w_ = min(step, ncols - c0)
            s = stg[i % 2]
            i += 1
            sv = s[:, 0:nk * w_].rearrange("p (k n) -> p k n", k=nk)
            dma("sync", s, sv, src, v[:, :, c0:c0 + w_])
            eng = "gpsimd" if i % 2 else "vector"
            op(eng, lambda e, sv=sv, c0=c0, w_=w_: e.tensor_copy(out=dst[:, :, c0:c0 + w_], in_=sv),
               reads=[s], writes=[dst], acc=True)

    def make_hT(src, row0, n, segs, bufs, it):
        xin = bufs["xin"][it % 2]
        xb = bufs["xb"][it % 2]
        hT = bufs["hT"][it % 2]
        nt = n // 128
        for (j0, ntl, st_, rw) in src(row0, n):
            dma("sync", xin, xin[:, j0:j0 + ntl, :], st_, st_.ap()[rw:rw + ntl * 128, :].rearrange("(j p) d -> p j d", p=128), acc=True)
        eng = "gpsimd" if (it % 2) else "vector"
        op(eng, lambda e: e.tensor_copy(out=xb[:, 0:nt, :], in_=xin[:, 0:nt, :]), reads=[xin], writes=[xb])
        for k in range(8):
            pt = pb[6 + (k % 2)]
            ptb = pt[:].bitcast(BF16)
            for j in range(nt):
                op("tensor", lambda e, k=k, j=j, ptb=ptb: e.transpose(out=ptb[:, j * 128:(j + 1) * 128],
                                                                     in_=xb[:, j, k * 128:(k + 1) * 128], identity=identb[:]),
                   reads=[xb, identb], writes=[pt], acc=True)
            for (c0, m, jj) in segs:
                op("scalar", lambda e, k=k, c0=c0, m=m, jj=jj, ptb=ptb: e.activation(
                    out=hT[:, k, c0:c0 + m], in_=ptb[:, c0:c0 + m], func=AF.Identity,
                    bias=modf[:, k, jj:jj + 1], scale=s1p[:, k, jj:jj + 1]),
                   reads=[pt, modf, s1p], writes=[hT], acc=True)
        return hT

    def proj(ps_t, rows, n, w_t, c0, src_t, src_fn, nk):
        for k in range(nk):
            op("tensor", lambda e, k=k: e.matmul(ps_t[0:rows, 0:n], lhsT=w_t[:, k, c0:c0 + rows], rhs=src_fn(k),
                                                 start=(k == 0), stop=(k == nk - 1)),
               reads=[w_t, src_t], writes=[ps_t], acc=True)

    def rope(out_t, out_ap, pa, pbk, rows, n, cos_t, cos_ap, sin_t, sin_ap, tmp):
        t1, t2 = tmp
        op("vector", lambda e: e.tensor_tensor(out=t1[0:rows, 0:n], in0=pa[0:rows, 0:n], in1=cos_ap, op=ALU.mult),
           reads=[pa, cos_t], writes=[t1])
        op("vector", lambda e: e.tensor_tensor(out=t2[0:rows, 0:n], in0=pbk[0:rows, 0:n], in1=sin_ap, op=ALU.mult),
           reads=[pbk, sin_t], writes=[t2])
        op("gpsimd", lambda e: e.tensor_tensor(out=out_ap, in0=t1[0:rows, 0:n], in1=t2[0:rows, 0:n], op=ALU.add),
           reads=[t1, t2], writes=[out_t])

    def rstd_bc(dst_t, ps_t, rows, n, inv_n):
        op("scalar", lambda e: e.activation(out=dst_t[0:rows, 0:n], in_=ps_t[0:rows, 0:n], func=AF.Sqrt, bias=epsc[0:rows, 0:1], scale=float(inv_n)),
           reads=[ps_t, epsc], writes=[dst_t])
        op("vector", lambda e: e.reciprocal(out=dst_t[0:rows, 0:n], in_=dst_t[0:rows, 0:n]), reads=[dst_t], writes=[dst_t])

    def phase_bufs():
        return dict(xin=[P.sb("xin%d" % i, [128, 4, D], F32) for i in range(2)],
                    xb=[P.sb("xb%d" % i, [128, 4, D], BF16) for i in range(2)],
                    hT=[P.sb("hT%d" % i, [128, 8, 512], BF16) for i in range(2)])

    def key_segs(r0, n):
        c_start = NK - CTX
        segs = []
        if r0 < c_start:
            m = min(n, c_start - r0)
            segs.append((0, m, 0))
            if m < n:
                segs.append((m, n - m, 1))
        else:
            segs.append((0, n, 1))
        return segs

    def q_segs(r0, n):
        return [(0, n, 0)] if r0 < TOK else [(0, n, 1)]

    P.off = persist_end
    bufs = phase_bufs()
    stg = [P.sb("stg%d" % i, [128, 2048], F32) for i in range(2)]
    tabc = [P.sb("tabc%d" % i, [128, 512], F32) for i in range(2)]
    tabs = [P.sb("tabs%d" % i, [128, 512], F32) for i in range(2)]
    tmp1 = [P.sb("tmpa%d" % i, [128, 512], F32) for i in range(2)]
    tmp2 = [P.sb("tmpb%d" % i, [128, 512], F32) for i in range(2)]
    kout = [P.sb("kout%d" % i, [128, 512], BF16) for i in range(3)]
    vout = [P.sb("vout%d" % i, [128, 1024], BF16) for i in range(2)]
    cnt = {"k": 0, "v": 0, "t": 0}

    def load_tab(tc_d, ts_d, r0, n, it, rows=128):
        a, b = tabc[it % 2], tabs[it % 2]
        dma("sync", a, a[0:rows, 0:n], tc_d, tc_d.ap()[0:rows, r0:r0 + n])
        dma("sync", b, b[0:rows, 0:n], ts_d, ts_d.ap()[0:rows, r0:r0 + n])
        return a, b

    def store_rows(dst_t, dst_ap, src_t, src_ap):
        dma("gpsimd", dst_t, dst_ap, src_t, src_ap, acc=True)

    def v_proj(src_t, src_fn, nk, w_t, c0, ncols, r0, n):
        for j in range(n // 128):
            vo = vout[cnt["v"] % 2]
            cnt["v"] += 1
            for c in range(0, ncols, 512):
                m = min(512, ncols - c)
                pv = pb[4 + (c // 512) % 2]
                for k in range(nk):
                    op("tensor", lambda e, k=k, j=j, c=c, m=m, pv=pv: e.matmul(pv[:, 0:m], lhsT=src_fn(k, j), rhs=w_t[:, k, c0 + c:c0 + c + m],
                                                                          start=(k == 0), stop=(k == nk - 1)),
                       reads=[w_t, src_t], writes=[pv], acc=True)
                op("scalar", lambda e, c=c, m=m, pv=pv, vo=vo: e.copy(out=vo[:, c:c + m], in_=pv[:, 0:m]),
                   reads=[pv], writes=[vo], acc=True)
            store_rows(Vd, Vd.ap()[r0 + j * 128:r0 + (j + 1) * 128, :], vo, vo[:, 0:ncols])

    if L in (1, 2, 3):
        nkc = {1: 1024, 2: 256, 3: 128}[L]
        wk_sb = P.sb("wk_sb", [128, 8, 3 * nkc], BF16)
        load_weights(wk_sb, wK, 3 * nkc, 8, stg)
        sq = P.sb("sq", [128, 512], F32)
        rs = P.sb("rs", [128, 512], F32)
        for it, (r0, n) in enumerate(chunks(NK)):
            hT = make_hT(ksrc, r0, n, key_segs(r0, n), bufs, it)
            tc_, ts_ = load_tab(tkc, tks, r0, n, it)
            for t in range(nkc // 128):
                pa, pbk = pb[0 + 2 * (t % 2)], pb[1 + 2 * (t % 2)]
                proj(pa, 128, n, wk_sb, t * 128, hT, lambda k, hT=hT, n=n: hT[:, k, 0:n], 8)
                proj(pbk, 128, n, wk_sb, nkc + t * 128, hT, lambda k, hT=hT, n=n: hT[:, k, 0:n], 8)
                ko = kout[cnt["k"] % 3]
                cnt["k"] += 1
                tm = (tmp1[cnt["t"] % 2], tmp2[cnt["t"] % 2])
                cnt["t"] += 1
                if L == 2:
                    op("scalar", lambda e, pa=pa, n=n: e.activation(out=sq[:, 0:n], in_=pa[:, 0:n], func=AF.Square), reads=[pa], writes=[sq])
                    op("tensor", lambda e, n=n: e.matmul(pb[4][:, 0:n], lhsT=onesf[:], rhs=sq[:, 0:n], start=True, stop=True),
                       reads=[onesf, sq], writes=[pb[4]])
                    rstd_bc(rs, pb[4], 128, n, 1.0 / 128)
                    op("vector", lambda e, tm=tm, pa=pa, tc_=tc_, n=n: e.scalar_tensor_tensor(
                        out=tm[0][:, 0:n], in0=pa[:, 0:n], scalar=small[:, 2:3], in1=tc_[:, 0:n], op0=ALU.mult, op1=ALU.mult),
                       reads=[pa, tc_, small], writes=[tm[0]])
                    op("vector", lambda e, tm=tm, pbk=pbk, ts_=ts_, n=n: e.scalar_tensor_tensor(
                        out=tm[1][:, 0:n], in0=pbk[:, 0:n], scalar=small[:, 3:4], in1=ts_[:, 0:n], op0=ALU.mult, op1=ALU.mult),
                       reads=[pbk, ts_, small], writes=[tm[1]])
                    op("gpsimd", lambda e, tm=tm, n=n: e.tensor_tensor(out=tm[0][:, 0:n], in0=tm[0][:, 0:n], in1=tm[1][:, 0:n], op=ALU.add),
                       reads=[tm[0], tm[1]], writes=[tm[0]])
                    op("vector", lambda e, tm=tm, ko=ko, n=n: e.tensor_tensor(out=ko[:, 0:n], in0=tm[0][:, 0:n], in1=rs[:, 0:n], op=ALU.mult),
                       reads=[tm[0], rs], writes=[ko])
                else:
                    rope(ko, ko[:, 0:n], pa, pbk, 128, n, tc_, tc_[:, 0:n], ts_, ts_[:, 0:n], tm)
                    pass
                store_rows(KT, KT.ap()[t * 128:(t + 1) * 128, r0:r0 + n], ko, ko[:, 0:n])
            v_proj(hT, lambda k, j, hT=hT: hT[:, k, j * 128:(j + 1) * 128], 8, wk_sb, 2 * nkc, nkc, r0, n)
    else:
        wa_sb = P.sb("wa_sb", [128, 8, 448], BF16)
        wkn_sb = P.sb("wkn_sb", [128, 1, 1024], BF16)
        wv_sb = P.sb("wv_sb", [128, 1, 1024], BF16)
        load_weights(wa_sb, wA, 448, 8, stg)
        load_weights(wkn_sb, wkn, 1024, 1, stg)
        load_weights(wv_sb, wv, 1024, 1, stg)
        sq = P.sb("sq", [128, 512], F32)
        rs = P.sb("rs", [128, 512], F32)
        kvn = [P.sb("kvn%d" % i, [128, 512], BF16) for i in range(2)]
        for it, (r0, n) in enumerate(chunks(NK)):
            hT = make_hT(ksrc, r0, n, key_segs(r0, n), bufs, it)
            tc_, ts_ = load_tab(tkc, tks, r0, n, it, rows=32)
            hsrc = lambda k, hT=hT, n=n: hT[:, k, 0:n]
            proj(pb[0], 128, n, wa_sb, 256, hT, hsrc, 8)
            op("scalar", lambda e, n=n: e.activation(out=sq[:, 0:n], in_=pb[0][:, 0:n], func=AF.Square), reads=[pb[0]], writes=[sq])
            op("tensor", lambda e, n=n: e.matmul(pb[4][:, 0:n], lhsT=onesf[:], rhs=sq[:, 0:n], start=True, stop=True),
               reads=[onesf, sq], writes=[pb[4]])
            rstd_bc(rs, pb[4], 128, n, 1.0 / 128)
            kv = kvn[it % 2]
            op("vector", lambda e, kv=kv, n=n: e.scalar_tensor_tensor(out=kv[:, 0:n], in0=pb[0][:, 0:n], scalar=small[:, 2:3], in1=rs[:, 0:n],
                                                                    op0=ALU.mult, op1=ALU.mult), reads=[pb[0], rs, small], writes=[kv])
            proj(pb[1], 32, n, wa_sb, 384, hT, hsrc, 8)
            proj(pb[2], 32, n, wa_sb, 416, hT, hsrc, 8)
            ko = kout[cnt["k"] % 3]
            cnt["k"] += 1
            tm = (tmp1[cnt["t"] % 2], tmp2[cnt["t"] % 2])
            cnt["t"] += 1
            rope(ko, ko[0:32, 0:n], pb[1], pb[2], 32, n, tc_, tc_[0:32, 0:n], ts_, ts_[0:32, 0:n], tm)
            store_rows(KPE, KPE.ap()[:, r0:r0 + n], ko, ko[0:32, 0:n])
            for t in range(8):
                pa = pb[2 + (t % 2)] if False else pb[3 if t % 2 else 1]
                proj(pa, 128, n, wkn_sb, t * 128, kv, lambda k, kv=kv, n=n: kv[:, 0:n], 1)
                ko = kout[cnt["k"] % 3]
                cnt["k"] += 1
                op("scalar", lambda e, ko=ko, pa=pa, n=n: e.copy(out=ko[:, 0:n], in_=pa[:, 0:n]), reads=[pa], writes=[ko])
                store_rows(KN, KN.ap()[t * 128:(t + 1) * 128, r0:r0 + n], ko, ko[:, 0:n])
            v_proj(kv, lambda k, j, kv=kv: kv[:, j * 128:(j + 1) * 128], 1, wv_sb, 0, 1024, r0, n)

    P.barrier()

    P.off = persist_end
    bufs = phase_bufs()
    stg = [P.sb("stg%d" % i, [128, 2048], F32) for i in range(2)]
    tabc = [P.sb("tabc%d" % i, [128, 512], F32) for i in range(2)]
    tabs = [P.sb("tabs%d" % i, [128, 512], F32) for i in range(2)]
    tmp1 = [P.sb("tmpa%d" % i, [128, 512], F32) for i in range(2)]
    tmp2 = [P.sb("tmpb%d" % i, [128, 512], F32) for i in range(2)]
    kout = [P.sb("kout%d" % i, [128, 512], BF16) for i in range(3)]
    sq = P.sb("sq", [128, 512], F32)
    rs = P.sb("rs", [128, 512], F32)
    cnt = {"k": 0, "v": 0, "t": 0}
    qchunks = chunks(NQL)

    def gate_proj(hT, n, w_t, c0, r0):
        for t in range(8):
            pa = pb[4 + (t % 2)]
            proj(pa, 128, n, w_t, c0 + t * 128, hT, lambda k, hT=hT, n=n: hT[:, k, 0:n], 8)
            ko = kout[cnt["k"] % 3]
            cnt["k"] += 1
            op("scalar", lambda e, ko=ko, pa=pa, n=n: e.activation(out=ko[:, 0:n], in_=pa[:, 0:n], func=AF.Silu), reads=[pa], writes=[ko])
            store_rows(GT, GT.ap()[t * 128:(t + 1) * 128, r0:r0 + n], ko, ko[:, 0:n])

    if L in (1, 2, 3):
        wq_sb = P.sb("wq_sb", [128, 8, 3072], BF16)
        load_weights(wq_sb, wQ, 3072, 8, stg)
        for it, (r0, n) in enumerate(qchunks):
            hT = make_hT(qsrc, r0, n, q_segs(r0, n), bufs, it)
            tc_, ts_ = load_tab(tqc, tqs, r0, n, it)
            for t in range(8):
                pa, pbk = pb[0 + 2 * (t % 2)], pb[1 + 2 * (t % 2)]
                proj(pa, 128, n, wq_sb, t * 128, hT, lambda k, hT=hT, n=n: hT[:, k, 0:n], 8)
                proj(pbk, 128, n, wq_sb, 1024 + t * 128, hT, lambda k, hT=hT, n=n: hT[:, k, 0:n], 8)
                ko = kout[cnt["k"] % 3]
                cnt["k"] += 1
                tm = (tmp1[cnt["t"] % 2], tmp2[cnt["t"] % 2])
                cnt["t"] += 1
                if L == 2:
                    op("scalar", lambda e, pa=pa, n=n: e.activation(out=sq[:, 0:n], in_=pa[:, 0:n], func=AF.Square), reads=[pa], writes=[sq])
                    op("tensor", lambda e, n=n: e.matmul(pb[6][:, 0:n], lhsT=onesf[:], rhs=sq[:, 0:n], start=True, stop=True),
                       reads=[onesf, sq], writes=[pb[6]])
                    rstd_bc(rs, pb[6], 128, n, 1.0 / 128)
                    op("vector", lambda e, tm=tm, pa=pa, tc_=tc_, n=n: e.scalar_tensor_tensor(
                        out=tm[0][:, 0:n], in0=pa[:, 0:n], scalar=small[:, 0:1], in1=tc_[:, 0:n], op0=ALU.mult, op1=ALU.mult),
                       reads=[pa, tc_, small], writes=[tm[0]])
                    op("vector", lambda e, tm=tm, pbk=pbk, ts_=ts_, n=n: e.scalar_tensor_tensor(
                        out=tm[1][:, 0:n], in0=pbk[:, 0:n], scalar=small[:, 1:2], in1=ts_[:, 0:n], op0=ALU.mult, op1=ALU.mult),
                       reads=[pbk, ts_, small], writes=[tm[1]])
                    op("gpsimd", lambda e, tm=tm, n=n: e.tensor_tensor(out=tm[0][:, 0:n], in0=tm[0][:, 0:n], in1=tm[1][:, 0:n], op=ALU.add),
                       reads=[tm[0], tm[1]], writes=[tm[0]])
                    op("vector", lambda e, tm=tm, ko=ko, n=n: e.tensor_tensor(out=ko[:, 0:n], in0=tm[0][:, 0:n], in1=rs[:, 0:n], op=ALU.mult),
                       reads=[tm[0], rs], writes=[ko])
                else:
                    rope(ko, ko[:, 0:n], pa, pbk, 128, n, tc_, tc_[:, 0:n], ts_, ts_[:, 0:n], tm)
                store_rows(QT, QT.ap()[t * 128:(t + 1) * 128, r0:r0 + n], ko, ko[:, 0:n])
            gate_proj(hT, n, wq_sb, 2048, r0)
    else:
        wa_sb = P.sb("wa_sb", [128, 8, 256], BF16)
        wg_sb = P.sb("wg_sb", [128, 8, 1024], BF16)
        wqb_sb = P.sb("wqb_sb", [128, 2, 3072], BF16)
        load_weights(wa_sb, wA, 256, 8, stg)
        load_weights(wg_sb, wG, 1024, 8, stg)
        load_weights(wqb_sb, wqb, 3072, 2, stg)
        sq2 = P.sb("sq2", [128, 2, 512], F32)
        qn = [P.sb("qn%d" % i, [128, 2, 512], BF16) for i in range(2)]
        for it, (r0, n) in enumerate(qchunks):
            hT = make_hT(qsrc, r0, n, q_segs(r0, n), bufs, it)
            tc_, ts_ = load_tab(tqc, tqs, r0, n, it, rows=96)
            hsrc = lambda k, hT=hT, n=n: hT[:, k, 0:n]
            proj(pb[0], 128, n, wa_sb, 0, hT, hsrc, 8)
            proj(pb[1], 128, n, wa_sb, 128, hT, hsrc, 8)
            for u in range(2):
                op("scalar", lambda e, u=u, n=n: e.activation(out=sq2[:, u, 0:n], in_=pb[u][:, 0:n], func=AF.Square),
                   reads=[pb[u]], writes=[sq2], acc=True)
            for u in range(2):
                op("tensor", lambda e, u=u, n=n: e.matmul(pb[4][:, 0:n], lhsT=onesf[:], rhs=sq2[:, u, 0:n], start=(u == 0), stop=(u == 1)),
                   reads=[onesf, sq2], writes=[pb[4]], acc=True)
            rstd_bc(rs, pb[4], 128, n, 1.0 / 256)
            q_ = qn[it % 2]
            for u in range(2):
                op("vector", lambda e, u=u, q_=q_, n=n: e.scalar_tensor_tensor(out=q_[:, u, 0:n], in0=pb[u][:, 0:n], scalar=small[:, u:u + 1],
                                                                             in1=rs[:, 0:n], op0=ALU.mult, op1=ALU.mult),
                   reads=[pb[u], rs, small], writes=[q_], acc=True)
            for h in range(16):
                pa, pbk = pb[0 + 2 * (h % 2)], pb[1 + 2 * (h % 2)]
                proj(pa, 96, n, wqb_sb, h * 96, q_, lambda k, q_=q_, n=n: q_[:, k, 0:n], 2)
                proj(pbk, 96, n, wqb_sb, 1536 + h * 96, q_, lambda k, q_=q_, n=n: q_[:, k, 0:n], 2)
                ko = kout[cnt["k"] % 3]
                cnt["k"] += 1
                tm = (tmp1[cnt["t"] % 2], tmp2[cnt["t"] % 2])
                cnt["t"] += 1
                rope(ko, ko[0:96, 0:n], pa, pbk, 96, n, tc_, tc_[0:96, 0:n], ts_, ts_[0:96, 0:n], tm)
                store_rows(QT, QT.ap()[h * 96:(h + 1) * 96, r0:r0 + n], ko, ko[0:96, 0:n])
            gate_proj(hT, n, wg_sb, 0, r0)

    P.barrier()

    P.off = persist_end
    NT = NK // 128
    ktb = [P.sb("ktb%d" % i, [128, NK], BF16) for i in range(2)]
    vtb = [P.sb("vtb%d" % i, [128, NT, 128], BF16) for i in range(2)]
    nqt = 2 if L == 1 else 1
    qtb = [[P.sb("qtb%d_%d" % (i, j), [128, NQ], BF16) for j in range(nqt)] for i in range(2)]
    if L == 1:
        for qq in qtb:
            for q_ in qq:
                op("gpsimd", lambda e, q_=q_: e.memset(q_[:], 0.0), writes=[q_])
    pT = [P.sb("pT%d" % i, [128, 512], BF16) for i in range(4)]
    fin_a = [P.sb("fin_a%d" % i, [128, 512], F32) for i in range(2)]
    fin_b = [P.sb("fin_b%d" % i, [128, 512], F32) for i in range(2)]
    fin_c = P.sb("fin_c", [128, 512], F32)
    fin_d = P.sb("fin_d", [128, 512], F32)
    oout = [P.sb("oout%d" % i, [128, 512], BF16) for i in range(2)]
    lacc = [(P.sb("laD%d" % i, [128, 512], F32), P.sb("laP%d" % i, [128, 512], F32)) for i in range(2)] if cfg["fin"] in ("diff", "dv128") else [(None, None)] * 2
    if L == 3:
        msk = P.sb("msk", [128, 14, 512], BF16)
        mskf = P.sb("mskf", [128, 14, 512], F32)
        dma("sync", mskf, mskf[:], wmask, wmask[:])
        op("vector", lambda e: e.tensor_copy(out=msk[:], in_=mskf[:]), reads=[mskf], writes=[msk])
    dv = cfg["groups"][0]["dv"]
    if dv == 64:
        for v_ in vtb:
            op("vector", lambda e, v_=v_: e.memset(v_[:, :, 64:128], 1.0), writes=[v_])
    sc_att = float(cfg["scale"])
    Sps = [pb[0], pb[1], pb[2]]
    acc = [pb[3], pb[4], pb[5], pb[6]]
    st = {"s": 0, "p": 0, "a": 0, "f": 0, "o": 0, "q": 0, "l": 0}

    def key_tiles(ci, r0):
        if L < 3:
            if r0 >= TOK:
                return [(NT - 2, None), (NT - 1, None)]
            return [(t, None) for t in range(NT)]
        res = []
        for rel in range(-1, 5):
            kt = 4 * ci + rel
            if 0 <= kt < 32:
                res.append((kt, rel + 1))
        if ci == 0:
            res += [(32 + e_, 6 + e_) for e_ in (1, 3, 5, 7)]
        if ci == 7:
            res += [(32 + e_, 6 + e_) for e_ in (0, 2, 4, 6)]
        res += [(40, None), (41, None)]
        return res

    for gi, g in enumerate(cfg["groups"]):
        kt_sb = ktb[gi % 2]
        v_sb = vtb[gi % 2]
        for (nm, r0k, rows, p0) in g["kparts"]:
            src = kd[nm]
            dma("sync", kt_sb, kt_sb[p0:p0 + rows, :], src, src.ap()[r0k:r0k + rows, :], acc=True)
        dvv = g["dv"]
        vsrc = Vd.ap()[:, g["vc0"]:g["vc0"] + dvv].rearrange("(t p) c -> p t c", p=128)
        nsp = 4 if NT >= 64 else 2
        per = (NT + nsp - 1) // nsp
        for s_ in range(nsp):
            t0, t1 = s_ * per, min(NT, (s_ + 1) * per)
            dma("sync", v_sb, v_sb[:, t0:t1, 0:dvv], Vd, vsrc[:, t0:t1, :], acc=True)
        for qt in g["qtiles"]:
            q_sbs = qtb[st["q"] % 2]
            st["q"] += 1
            loads = qt.get("loads") or [(0, qt["q0"], qt["rows"], 0)]
            for (qi_, q0_, rows_, p0_) in loads:
                dma("sync", q_sbs[qi_], q_sbs[qi_][p0_:p0_ + rows_, 0:NQL], QT, QT.ap()[q0_:q0_ + rows_, 0:NQL], acc=True)
            for ci, (r0, n) in enumerate(qchunks):
                tiles = key_tiles(ci, r0)
                maps = qt["maps"]
                accs = []
                for mi, mp in enumerate(maps):
                    if cfg["fin"] == "diff":
                        a_o, a_l = acc[2 * mi], acc[2 * mi + 1]
                    elif cfg["fin"] == "dv128":
                        a_o, a_l = acc[2 * (st["a"] % 2)], acc[2 * (st["a"] % 2) + 1]
                        st["a"] += 1
                    else:
                        a_o = acc[st["a"] % 4]
                        a_l = None
                        st["a"] += 1
                    accs.append((a_o, a_l))
                    pb_, dq = mp["pb"], mp["dq"]
                    q_sb = q_sbs[mp.get("qi", 0)]
                    nt_ = len(tiles)
                    pend = []

                    def issue_s(ti):
                        kt, mk = tiles[ti]
                        sp = Sps[st["s"] % 3]
                        st["s"] += 1
                        op("tensor", lambda e, kt=kt, sp=sp: e.matmul(sp[:, 0:n], lhsT=kt_sb[pb_:pb_ + dq, kt * 128:(kt + 1) * 128],
                                                                     rhs=q_sb[pb_:pb_ + dq, r0:r0 + n], start=True, stop=True),
                           reads=[kt_sb, q_sb], writes=[sp])
                        pt_ = pT[st["p"] % 4]
                        st["p"] += 1
                        op("scalar", lambda e, sp=sp, pt_=pt_: e.activation(out=pt_[:, 0:n], in_=sp[:, 0:n], func=AF.Exp, scale=sc_att),
                           reads=[sp], writes=[pt_])
                        if mk is not None:
                            op("vector", lambda e, pt_=pt_, mk=mk: e.tensor_tensor(out=pt_[:, 0:n], in0=pt_[:, 0:n], in1=msk[:, mk, 0:n], op=ALU.mult),
                               reads=[pt_, msk], writes=[pt_])
                        return (kt, pt_)

                    def issue_pv(ti, kt, pt_):
                        first, last = (ti == 0), (ti == nt_ - 1)
                        op("tensor", lambda e: e.matmul(a_o[:, 0:n], lhsT=v_sb[:, kt, :], rhs=pt_[:, 0:n], start=first, stop=last),
                           reads=[v_sb, pt_], writes=[a_o], acc=not first)
                        if a_l is not None:
                            eng_, la = ("gpsimd", laP) if (ti % 4 == 3) else ("vector", laD)
                            if not used[eng_]:
                                used[eng_] = True
                                op(eng_, lambda e: e.tensor_copy(out=la[:, 0:n], in_=pt_[:, 0:n]), reads=[pt_], writes=[la])
                            else:
                                op(eng_, lambda e: e.tensor_tensor(out=la[:, 0:n], in0=la[:, 0:n], in1=pt_[:, 0:n], op=ALU.add),
                                   reads=[pt_, la], writes=[la])

                    used = {"vector": False, "gpsimd": False}
                    laD, laP = lacc[st["l"] % 2]
                    st["l"] += 1
                    LOOK = 2
                    for ti in range(nt_ + LOOK):
                        if ti < nt_:
                            pend.append(issue_s(ti))
                        if ti >= LOOK:
                            kt, pt_ = pend[ti - LOOK]
                            issue_pv(ti - LOOK, kt, pt_)
                    if a_l is not None:
                        srcs_ = [la for (la, k_) in ((laD, "vector"), (laP, "gpsimd")) if used[k_]]
                        for si, la in enumerate(srcs_):
                            op("tensor", lambda e, la=la, si=si: e.matmul(a_l[:, 0:n], lhsT=onesf[:], rhs=la[:, 0:n],
                                                                         start=(si == 0), stop=(si == len(srcs_) - 1)),
                               reads=[onesf, la], writes=[a_l], acc=(si > 0))

                fa, fb = fin_a[st["f"] % 2], fin_b[st["f"] % 2]
                st["f"] += 1
                if cfg["fin"] == "dv128":
                    (a_o, a_l) = accs[0]
                    oo = oout[st["o"] % 2]
                    st["o"] += 1
                    op("vector", lambda e, a_l=a_l, fa=fa: e.reciprocal(out=fa[:, 0:n], in_=a_l[:, 0:n]), reads=[a_l], writes=[fa])
                    op("vector", lambda e, a_o=a_o, fa=fa, oo=oo: e.tensor_tensor(out=oo[:, 0:n], in0=a_o[:, 0:n], in1=fa[:, 0:n], op=ALU.mult),
                       reads=[a_o, fa], writes=[oo])
                    o0 = maps[0]["o0"]
                    store_rows(OT, OT.ap()[o0:o0 + 128, r0:r0 + n], oo, oo[:, 0:n])
                elif cfg["fin"] == "dv64":
                    for mi, mp in enumerate(maps):
                        (a_o, _) = accs[mi]
                        oo = oout[st["o"] % 2]
                        st["o"] += 1
                        fa = fin_a[st["f"] % 2]
                        st["f"] += 1
                        if L == 3:
                            hcol = mp["head"]
                            op("vector", lambda e, a_o=a_o, fa=fa, hcol=hcol: e.tensor_scalar(out=fa[64:128, 0:n], in0=a_o[64:128, 0:n],
                                                                                          scalar1=small[64:128, hcol:hcol + 1], scalar2=None, op0=ALU.add),
                               reads=[a_o, small], writes=[fa])
                            op("vector", lambda e, fa=fa: e.reciprocal(out=fa[64:128, 0:n], in_=fa[64:128, 0:n]), reads=[fa], writes=[fa])
                        else:
                            op("vector", lambda e, a_o=a_o, fa=fa: e.reciprocal(out=fa[64:128, 0:n], in_=a_o[64:128, 0:n]), reads=[a_o], writes=[fa])
                        op("vector", lambda e, a_o=a_o, fa=fa, oo=oo: e.tensor_tensor(out=oo[0:64, 0:n], in0=a_o[0:64, 0:n], in1=fa[64:128, 0:n], op=ALU.mult),
                           reads=[a_o, fa], writes=[oo])
                        o0 = mp["o0"]
                        store_rows(OT, OT.ap()[o0:o0 + 64, r0:r0 + n], oo, oo[0:64, 0:n])
                else:
                    (o1, l1), (o2, l2) = accs
                    oo = oout[st["o"] % 2]
                    st["o"] += 1
                    op("vector", lambda e: e.reciprocal(out=fa[:, 0:n], in_=l1[:, 0:n]), reads=[l1], writes=[fa])
                    op("vector", lambda e: e.tensor_tensor(out=fa[:, 0:n], in0=o1[:, 0:n], in1=fa[:, 0:n], op=ALU.mult), reads=[o1, fa], writes=[fa])
                    op("vector", lambda e: e.reciprocal(out=fb[:, 0:n], in_=l2[:, 0:n]), reads=[l2], writes=[fb])
                    op("vector", lambda e: e.tensor_tensor(out=fb[:, 0:n], in0=o2[:, 0:n], in1=fb[:, 0:n], op=ALU.mult), reads=[o2, fb], writes=[fb])
                    op("vector", lambda e: e.scalar_tensor_tensor(out=fa[:, 0:n], in0=fb[:, 0:n], scalar=small[:, 0:1], in1=fa[:, 0:n],
                                                                op0=ALU.mult, op1=ALU.add), reads=[fa, fb, small], writes=[fa])
                    op("gpsimd", lambda e: e.tensor_tensor(out=fin_c[:, 0:n], in0=fa[:, 0:n], in1=fa[:, 0:n], op=ALU.mult), reads=[fa], writes=[fin_c])
                    op("tensor", lambda e: e.matmul(pb[7][:, 0:n], lhsT=onesf[:], rhs=fin_c[:, 0:n], start=True, stop=True),
                       reads=[onesf, fin_c], writes=[pb[7]])
                    rstd_bc(fin_d, pb[7], 128, n, 1.0 / 128)
                    op("vector", lambda e: e.scalar_tensor_tensor(out=oo[:, 0:n], in0=fa[:, 0:n], scalar=small[:, 1:2], in1=fin_d[:, 0:n],
                                                                op0=ALU.mult, op1=ALU.mult), reads=[fa, fin_d, small], writes=[oo])
                    o0 = maps[0]["o0"]
                    store_rows(OT, OT.ap()[o0:o0 + 128, r0:r0 + n], oo, oo[:, 0:n])

    P.barrier()

    P.off = persist_end
    stg = [P.sb("stg%d" % i, [128, 2048], F32) for i in range(2)]
    wo_sb = P.sb("wo_sb", [128, 8, D], BF16)
    load_weights(wo_sb, out_w, D, 8, stg)
    lng = P.sb("lng", [128, D], F32)
    lnb = P.sb("lnb", [128, D], F32)
    dma("sync", lng, lng[:], ln_g, ln_g.ap()[0, :].partition_broadcast(128))
    dma("sync", lnb, lnb[:], ln_b, ln_b.ap()[0, :].partition_broadcast(128))
    otb = [P.sb("otb%d" % i, [128, 8, 512], BF16) for i in range(2)]
    gtb = [P.sb("gtb%d" % i, [128, 8, 512], BF16) for i in range(2)]
    ogb = [P.sb("ogb%d" % i, [128, 8, 512], BF16) for i in range(2)]
    xres = [P.sb("xres%d" % i, [128, D], F32) for i in range(2)]
    yt = [P.sb("yt%d" % i, [128, D], F32) for i in range(2)]
    stt = [P.sb("stt%d" % i, [128, 16], F32) for i in range(2)]
    ti_ = 0
    for it, (r0, n) in enumerate(qchunks):
        ot, gt, og = otb[it % 2], gtb[it % 2], ogb[it % 2]
        dma("sync", ot, ot[:, :, 0:n], OT, OT.ap()[:, r0:r0 + n].rearrange("(k p) t -> p k t", p=128))
        dma("sync", gt, gt[:, :, 0:n], GT, GT.ap()[:, r0:r0 + n].rearrange("(k p) t -> p k t", p=128))
        op("gpsimd", lambda e, ot=ot, gt=gt, og=og, n=n: e.tensor_tensor(out=og[:, :, 0:n], in0=ot[:, :, 0:n], in1=gt[:, :, 0:n], op=ALU.mult),
           reads=[ot, gt], writes=[og])
        jj = 0 if r0 < TOK else 1
        for j in range(n // 128):
            xr, y, sv = xres[ti_ % 2], yt[ti_ % 2], stt[ti_ % 2]
            pa, pbk = pb[2 * (ti_ % 2)], pb[2 * (ti_ % 2) + 1]
            ti_ += 1
            row = r0 + j * 128
            (_, _, xsrc_t, xsrc_r) = qsrc(row, 128)[0]
            dma("sync", xr, xr[:], xsrc_t, xsrc_t.ap()[xsrc_r:xsrc_r + 128, :])
            for hf, pp in enumerate((pa, pbk)):
                for k in range(8):
                    op("tensor", lambda e, k=k, j=j, hf=hf, pp=pp, og=og: e.matmul(pp[:, :], lhsT=og[:, k, j * 128:(j + 1) * 128],
                                                                               rhs=wo_sb[:, k, hf * 512:(hf + 1) * 512], start=(k == 0), stop=(k == 7)),
                       reads=[og, wo_sb], writes=[pp], acc=True)
            for hf, pp in enumerate((pa, pbk)):
                op("vector", lambda e, hf=hf, pp=pp, y=y, jj=jj: e.tensor_tensor(out=y[:, hf * 512:(hf + 1) * 512], in0=pp[:, :],
                                                                             in1=gate_bc[jj][:, hf * 512:(hf + 1) * 512], op=ALU.mult),
                   reads=[pp, gate_bc[jj]], writes=[y], acc=True)
            op("vector", lambda e, y=y, xr=xr: e.scalar_tensor_tensor(out=y[:], in0=xr[:], scalar=float(ALPHA), in1=y[:], op0=ALU.mult, op1=ALU.add),
               reads=[xr, y], writes=[y])
            for hf in range(2):
                op("vector", lambda e, hf=hf, y=y, sv=sv: e.bn_stats(out=sv[:, hf * 6:(hf + 1) * 6], in_=y[:, hf * 512:(hf + 1) * 512]),
                   reads=[y], writes=[sv], acc=True)
            op("vector", lambda e, sv=sv: e.bn_aggr(out=sv[:, 12:14], in_=sv[:, 0:12]), reads=[sv], writes=[sv])
            op("scalar", lambda e, sv=sv: e.activation(out=sv[:, 14:15], in_=sv[:, 13:14], func=AF.Sqrt, bias=epsc[:, 0:1], scale=1.0),
               reads=[sv, epsc], writes=[sv])
            op("vector", lambda e, sv=sv: e.reciprocal(out=sv[:, 14:15], in_=sv[:, 14:15]), reads=[sv], writes=[sv])
            op("vector", lambda e, y=y, sv=sv: e.tensor_scalar(out=y[:], in0=y[:], scalar1=sv[:, 12:13], scalar2=sv[:, 14:15],
                                                             op0=ALU.subtract, op1=ALU.mult), reads=[y, sv], writes=[y])
            op("gpsimd", lambda e, y=y: e.tensor_tensor(out=y[:], in0=y[:], in1=lng[:], op=ALU.mult), reads=[y, lng], writes=[y])
            op("gpsimd", lambda e, y=y: e.tensor_tensor(out=y[:], in0=y[:], in1=lnb[:], op=ALU.add), reads=[y, lnb], writes=[y])
            od_t, od_r = odst(row)
            dma("gpsimd", od_t, od_t.ap()[od_r:od_r + 128, :], y, y[:], acc=True)
    if L < 3:
        for c_ in range(TOK // 256):
            op("gpsimd", lambda e, c_=c_: e.collective_compute(
                "AllGather", ALU.bypass, replica_groups=[[0, 1, 2, 3], [4, 5, 6, 7]],
                ins=[XO[L].ap()[c_ * 256:(c_ + 1) * 256, :].opt()], outs=[GX[L].ap()[c_ * 1024:(c_ + 1) * 1024, :].opt()]),
               reads=[XO[L]], writes=[GX[L]], dma=GX[L], dma_inc=1, acc=True)
    else:
        op("gpsimd", lambda e: e.nop(), reads=[sh["out"]])
    P.barrier()


def build_all():
    nc = bass.Bass("TRN2", target_bir_lowering=False)
    P = Prog(nc)
    sh = dict(pb=[P.ps("pb%d" % i, [128, 512], F32) for i in range(8)], sb0=P.off)
    sh["XO"] = [P.dram("XO%d" % i, [TOK, D], F32) for i in range(3)]
    sh["CX"] = [P.dram("CX%d" % i, [CTX, D], F32) for i in range(3)]
    sh["GX"] = [P.dram("GX%d" % i, [SEQ, D], F32) for i in range(3)]
    sh["out"] = P.dram("out", [TOK, D], F32, kind="ExternalOutput")
    for L in range(4):
        emit_layer(nc, P, L, sh)
    P.emit()
    return nc


def rope_table(positions, rot_dim):
    pos = np.asarray(positions)
    row = (pos // GRID_W).astype(np.float32)
    col = (pos % GRID_W).astype(np.float32)
    n_freq = rot_dim // 4
    inv = (np.float32(THETA) ** (-np.arange(n_freq, dtype=np.float32) / np.float32(n_freq))).astype(np.float32)
    ang = np.concatenate([row[:, None] * inv[None, :], col[:, None] * inv[None, :]], axis=-1).astype(np.float32)
    cos = np.cos(ang).astype(np.float32).T
    sin = np.sin(ang).astype(np.float32).T
    c = np.concatenate([cos, cos], axis=0)
    s = np.concatenate([-sin, sin], axis=0)
    return c, s


def swap_halves(w, head_dim):
    n = w.shape[-1]
    idx = np.arange(n).reshape(-1, head_dim)
    half = head_dim // 2
    idx = np.concatenate([idx[:, half:], idx[:, :half]], axis=1).reshape(-1)
    return w[..., idx]


def tile_tab(c, s, reps, ntok_ctx):
    c = np.concatenate([np.tile(c, (reps, 1)), np.ones((c.shape[0] * reps, ntok_ctx), np.float32)], axis=1)
    s = np.concatenate([np.tile(s, (reps, 1)), np.zeros((s.shape[0] * reps, ntok_ctx), np.float32)], axis=1)
    return c, s


def pad_rows(a, rows=128, fill=0.0):
    if a.shape[0] == rows:
        return np.ascontiguousarray(a)
    out = np.full((rows, a.shape[1]), fill, np.float32)
    out[:a.shape[0]] = a
    return out


_PROG = None


def get_prog():
    global _PROG
    if _PROG is None:
        _PROG = build_all()
    return _PROG


def fm(v):
    return np.ascontiguousarray(np.asarray(v, np.float32).reshape(8, 128).T)


def layer_inputs(L, inp):
    maps = []
    f32 = np.float32
    ada_w = np.ascontiguousarray(inp["ada_w"][L])
    ada_b = inp["ada_b"][L]
    ada_bf = np.concatenate([fm(ada_b[0:1024]), fm(ada_b[1024:2048])], axis=1)
    ada_bg = np.ascontiguousarray(ada_b[2048:3072][None, :])
    common = dict(ada_w=ada_w, ada_bf=np.ascontiguousarray(ada_bf), ada_bg=ada_bg,
                  out_w=np.ascontiguousarray(inp["out_w"][L]), ln_g=np.ascontiguousarray(inp["ln_g"][L][None, :]),
                  ln_b=np.ascontiguousarray(inp["ln_b"][L][None, :]), ident=np.eye(128, dtype=f32))
    if L == 0:
        w_in = inp["mla_w_in"][0]
        kpe = w_in[:, 384:416]
        common["wA"] = np.ascontiguousarray(np.concatenate([w_in[:, 0:384], kpe, swap_halves(kpe, 32)], axis=1))
        common["wG"] = np.ascontiguousarray(w_in[:, 416:1440])
        wqb = inp["mla_w_qb"][0].reshape(256, 16, 96)
        wqb_sw = np.concatenate([wqb[:, :, 0:64], swap_halves(wqb[:, :, 64:96], 32)], axis=2)
        common["wqb"] = np.ascontiguousarray(np.concatenate([wqb.reshape(256, 1536), wqb_sw.reshape(256, 1536)], axis=1))
        wkvb = inp["mla_w_kvb"][0].reshape(128, 16, 128)
        common["wkn"] = np.ascontiguousarray(wkvb[:, :, 0:64].reshape(128, 1024))
        common["wv"] = np.ascontiguousarray(wkvb[:, :, 64:128].reshape(128, 1024))
        common["gq"] = np.ascontiguousarray(inp["mla_g_qa"][0].reshape(2, 128).T)
        common["gkv"] = np.ascontiguousarray(inp["mla_g_kva"][0].reshape(1, 128).T)
    elif L == 1:
        w_in = inp["diff_w_in"][0]
        q, k, v, g = w_in[:, 0:1024], w_in[:, 1024:2048], w_in[:, 2048:3072], w_in[:, 3072:4096]
        common["wK"] = np.ascontiguousarray(np.concatenate([k, swap_halves(k, 64), v], axis=1))
        common["wQ"] = np.ascontiguousarray(np.concatenate([q, swap_halves(q, 64), g], axis=1))
        common["lam"] = np.ascontiguousarray(inp["diff_lambda"][0].reshape(1, 256))
        common["gsub"] = np.ascontiguousarray(inp["diff_g_sub"][0].reshape(128, 1))
    elif L == 2:
        w_in = inp["gqa_w_in"][0]
        q, k, v, g = w_in[:, 0:1024], w_in[:, 1024:1280], w_in[:, 1280:1536], w_in[:, 1536:2560]
        common["wK"] = np.ascontiguousarray(np.concatenate([k, swap_halves(k, 128), v], axis=1))
        common["wQ"] = np.ascontiguousarray(np.concatenate([q, swap_halves(q, 128), g], axis=1))
        gq_, gk_ = inp["gqa_g_q"][0], inp["gqa_g_k"][0]
        common["gqk"] = np.ascontiguousarray(np.stack([gq_, swap_halves(gq_[None], 128)[0], gk_, swap_halves(gk_[None], 128)[0]], axis=1))
    else:
        w_in = inp["swa_w_in"][0]
        q, k, v, g = w_in[:, 0:1024], w_in[:, 1024:1152], w_in[:, 1152:1280], w_in[:, 1280:2304]
        common["wK"] = np.ascontiguousarray(np.concatenate([k, swap_halves(k, 64), v], axis=1))
        common["wQ"] = np.ascontiguousarray(np.concatenate([q, swap_halves(q, 64), g], axis=1))
        common["sink"] = np.ascontiguousarray(inp["swa_sink"][0].reshape(1, 16))
    rot = {0: 32, 1: 64, 2: 128, 3: 64}[L]
    reps = {0: 1, 1: 2, 2: 1, 3: 2}[L]
    for r in range(NCORE):
        b, qr = r // 4, r % 4
        t0 = qr * TOK
        m = dict(common)
        cv = np.stack([fm(inp["c"][b]), fm(inp["c_ctx"])], axis=2)
        m["cvec"] = np.ascontiguousarray(cv)
        qpos = np.arange(t0, t0 + TOK)
        if L == 0:
            own = inp["x"][b, t0:t0 + TOK]
            m["xq"] = np.ascontiguousarray(np.concatenate([own, inp["ctx"][b]], axis=0))
            m["xs"] = np.ascontiguousarray(np.concatenate([inp["x"][b], inp["ctx"][b]], axis=0))
        if L < 3:
            kpos = np.arange(SEQ)
        else:
            edge = []
            for e_ in range(8):
                base = (e_ // 2) * TOK + (0 if e_ % 2 == 0 else TOK - 128)
                edge.append(np.arange(base, base + 128))
            kpos = np.concatenate([qpos] + edge)
            jl = np.arange(128)[:, None]
            il = np.arange(128)[None, :]
            lo = (jl >= il).astype(f32)
            up = (jl <= il).astype(f32)
            full = np.ones((128, 128), f32)
            zero = np.zeros((128, 128), f32)
            wm = np.zeros((128, 14, 512), f32)
            for rel in range(-1, 5):
                blocks = []
                for a in range(4):
                    d_ = rel - a
                    blocks.append(lo if d_ == -1 else full if d_ == 0 else up if d_ == 1 else zero)
                wm[:, rel + 1, :] = np.concatenate(blocks, axis=1)
            if qr > 0:
                wm[:, 6 + 2 * (qr - 1) + 1, :] = wm[:, 0, :]
            if qr < 3:
                wm[:, 6 + 2 * (qr + 1), :] = wm[:, 5, :]
            m["wmask"] = wm
        ck, sk = rope_table(kpos, rot)
        cq, sq_ = rope_table(qpos, rot)
        if L == 0:
            cqf = np.concatenate([np.ones((64, TOK), f32), cq], axis=0)
            sqf = np.concatenate([np.zeros((64, TOK), f32), sq_], axis=0)
            cq, sq_ = tile_tab(cqf, sqf, 1, CTX)
            ck, sk = tile_tab(ck, sk, 1, CTX)
        else:
            cq, sq_ = tile_tab(cq, sq_, reps, CTX)
            ck, sk = tile_tab(ck, sk, reps, CTX)
        m["tkc"], m["tks"] = pad_rows(ck), pad_rows(sk)
        m["tqc"], m["tqs"] = pad_rows(cq), pad_rows(sq_)
        maps.append(m)
    return maps


def kernel(**inputs):
    inp = {k: np.ascontiguousarray(np.asarray(v), dtype=np.float32) for k, v in inputs.items()}
    nc = get_prog()
    in_maps = [dict() for _ in range(NCORE)]
    for L in range(4):
        lm = layer_inputs(L, inp)
        for r in range(NCORE):
            for k, v in lm[r].items():
                in_maps[r]["l%d_%s" % (L, k)] = v
    res = run_bass_kernel_spmd(nc, in_maps, core_ids=list(range(NCORE)))
    out = np.empty((2, SEQ, D), np.float32)
    for r in range(NCORE):
        b, qr = r // 4, r % 4
        out[b, qr * TOK:(qr + 1) * TOK] = np.asarray(res.results[r]["out"]).reshape(TOK, D)
    return out
```

```python
import math
import types
import numpy as np
import ml_dtypes
import concourse.bass as bass
import concourse.mybir as mybir
from concourse.bass_utils import run_bass_kernel_spmd

F32 = mybir.dt.float32
BF16 = mybir.dt.bfloat16
AF = mybir.ActivationFunctionType
ALU = mybir.AluOpType

ENGS = ("tensor", "scalar", "vector", "gpsimd", "sync")

D = 1024
SEQ = 16384
CTX = 256
NCORE = 8
TOK = 4096
NQ = TOK + CTX
NKF = SEQ + CTX
NK3 = TOK + 8 * 128 + CTX
EPS = 1e-6
ALPHA = (2 * 4) ** 0.25
GRID_W = 64
THETA = 10000.0


class T:
    __slots__ = ("h", "name", "w", "r", "dsem", "dcnt", "keep")

    def __init__(self, h, name):
        self.h = h
        self.name = name
        self.w = {}
        self.r = {}
        self.dsem = None
        self.dcnt = 0
        self.keep = False

    def __getitem__(self, k):
        return self.h[k]

    def ap(self):
        return self.h.ap()


class Ins:
    __slots__ = ("eng", "fn", "waits", "idx", "inc", "val", "dsem", "dinc")

    def __init__(self, eng, fn):
        self.eng = eng
        self.fn = fn
        self.waits = []
        self.idx = 0
        self.inc = False
        self.val = 0
        self.dsem = None


def freeze(fn):
    if fn.__closure__ is None:
        return fn
    cells = []
    for c in fn.__closure__:
        try:
            cells.append(types.CellType(c.cell_contents))
        except ValueError:
            cells.append(c)
    return types.FunctionType(fn.__code__, fn.__globals__, fn.__name__, fn.__defaults__, tuple(cells))


class Prog:
    def __init__(self, nc):
        self.nc = nc
        self.q = {e: [] for e in ENGS}
        self.waited = {e: {} for e in ENGS}
        self.esem = {}
        self.dsems = []
        self.semobj = {}
        self.off = 16512
        self.uid = 0
        self.sem_pool = []
        self.scope = []

    def sb(self, name, shape, dt):
        nbytes = int(np.prod(shape[1:])) * (2 if dt == BF16 else 4)
        nbytes = (nbytes + 63) // 64 * 64
        self.uid += 1
        h = self.nc.alloc_sbuf_tensor_at(f"{name}_{self.uid}", list(shape), dt, offset=self.off)
        self.off += nbytes
        assert self.off <= 229344, ("SBUF overflow", name, self.off)
        t = T(h, name)
        self.scope.append(t)
        return t

    def ps(self, name, shape, dt=F32):
        return T(self.nc.alloc_psum_tensor(name, list(shape), dt), name)

    def dram(self, name, shape, dt, kind="Internal"):
        return T(self.nc.dram_tensor(name, list(shape), dt, kind=kind), name)

    def op(self, eng, fn, reads=(), writes=(), dma=None, acc=False, dma_inc=16):
        ins = Ins(eng, freeze(fn))
        lst = self.q[eng]
        ins.idx = len(lst)
        deps = {}

        def add(d, skipkey=None):
            for k, vo in d.items():
                if k == skipkey:
                    continue
                if k not in deps or deps[k][0] < vo[0]:
                    deps[k] = vo

        if dma is not None:
            if dma.dsem is None:
                if self.sem_pool:
                    dma.dsem, dma.dcnt = self.sem_pool.pop()
                else:
                    self.nsem = getattr(self, "nsem", 0) + 1
                    dma.dsem = self.nc.alloc_semaphore("d_%s_%d" % (dma.name, self.nsem))
                self.dsems.append(dma)
            mykey = ("d", id(dma))
            self.semobj[mykey] = dma
        else:
            mykey = eng
        for t in reads:
            add(t.w)
        for t in writes:
            add(t.w, mykey if acc else None)
            add(t.r)
        waited = self.waited[eng]
        for k, (v, o) in deps.items():
            if k == "tensor" and eng == "tensor":
                continue
            if waited.get(k, -1) >= v:
                continue
            waited[k] = v
            ins.waits.append((k, v, o, None if o is not None else self.semobj[k].dsem))
            if o is not None:
                o.inc = True
        if dma is not None:
            dma.dcnt += dma_inc
            ins.dsem = dma.dsem
            ins.dinc = dma_inc
            comp = (dma.dcnt, None)
        else:
            comp = (ins.idx, ins)
        for t in reads:
            cur = t.r.get(mykey)
            if cur is None or cur[0] < comp[0]:
                t.r[mykey] = comp
        for t in writes:
            if acc:
                t.w[mykey] = comp
            else:
                t.w = {mykey: comp}
            t.r = {}
        lst.append(ins)
        return ins

    def barrier(self, release=True):
        marks = []
        for e in ENGS:
            last = None
            for ins in reversed(self.q[e]):
                if ins.dsem is None and not getattr(ins.fn, "_isnop", False):
                    last = ins
                    break
            if last is not None:
                m = T(None, "bar_" + e)
                m.w = {e: (last.idx, last)}
                marks.append(m)
        sm = T(None, "bar_sync")
        nop1 = lambda eng: eng.nop()
        ins = self.op("sync", nop1, reads=marks, writes=[sm])
        ins.fn._isnop = True
        waited = self.waited["sync"]
        for t in self.dsems:
            k = ("d", id(t))
            if waited.get(k, -1) < t.dcnt:
                waited[k] = t.dcnt
                ins.waits.append((k, t.dcnt, None, t.dsem))
        for e in ENGS:
            if e != "sync":
                i2 = self.op(e, lambda eng: eng.nop(), reads=marks + [sm])
                i2.fn._isnop = True
        for e in ENGS:
            for t in self.dsems:
                k = ("d", id(t))
                if self.waited[e].get(k, -1) < t.dcnt:
                    self.waited[e][k] = t.dcnt
        if release:
            for t in self.scope:
                if t.dsem is not None and not getattr(t, "keep", False):
                    self.sem_pool.append((t.dsem, t.dcnt))
                    self.dsems.remove(t)
                    t.dsem = None
            self.scope = [t for t in self.scope if getattr(t, "keep", False)]

    def emit(self):
        nc = self.nc
        for e in ENGS:
            c = 0
            for ins in self.q[e]:
                if ins.inc:
                    c += 1
                ins.val = c
        for e in ENGS:
            self.esem[e] = nc.alloc_semaphore("e_" + e)

        def run(engname, eng):
            for ins in self.q[engname]:
                for (k, v, o, hnd) in ins.waits:
                    if o is not None:
                        eng.wait_ge(self.esem[k], o.val)
                    else:
                        eng.wait_ge(hnd, v)
                r = ins.fn(eng)
                if ins.dsem is not None:
                    r.then_inc(ins.dsem, ins.dinc)
                elif ins.inc:
                    r.then_inc(self.esem[engname], 1)

        with nc.Block() as block:
            @block.tensor
            def _(eng):
                run("tensor", eng)

            @block.scalar
            def _(eng):
                run("scalar", eng)

            @block.vector
            def _(eng):
                run("vector", eng)

            @block.gpsimd
            def _(eng):
                run("gpsimd", eng)

            @block.sync
            def _(eng):
                run("sync", eng)


def layer_cfg(L):
    c = {}
    if L == 0:
        c.update(nk=NKF, nq=NQ, scale=96 ** -0.5, vcols=1024, qrows=16 * 96, krows=1024, fin="dv64")
        c["groups"] = [dict(kparts=[("KN", h * 64, 64, 0), ("KPE", 0, 32, 64)], vc0=h * 64, dv=64,
                            qtiles=[dict(q0=h * 96, rows=96, maps=[dict(pb=0, dq=96, o0=h * 64)])])
                       for h in range(16)]
    elif L == 1:
        c.update(nk=NKF, nq=NQ, scale=64 ** -0.5, vcols=1024, qrows=1024, krows=1024, fin="diff")
        c["groups"] = [dict(kparts=[("KT", h * 128, 128, 0)], vc0=h * 128, dv=128,
                            qtiles=[dict(loads=[(0, h * 128, 64, 0), (1, h * 128 + 64, 64, 64)],
                                         maps=[dict(qi=0, pb=0, dq=128, o0=h * 128), dict(qi=1, pb=0, dq=128, o0=h * 128)])])
                       for h in range(8)]
    elif L == 2:
        c.update(nk=NKF, nq=NQ, scale=128 ** -0.5, vcols=256, qrows=1024, krows=256, fin="dv128")
        c["groups"] = [dict(kparts=[("KT", g * 128, 128, 0)], vc0=g * 128, dv=128,
                            qtiles=[dict(q0=hq * 128, rows=128, maps=[dict(pb=0, dq=128, o0=hq * 128)])
                                    for hq in range(4 * g, 4 * g + 4)])
                       for g in range(2)]
    else:
        c.update(nk=NK3, nq=TOK, scale=64 ** -0.5, vcols=128, qrows=1024, krows=128, fin="dv64")
        c["groups"] = [dict(kparts=[("KT", g * 64, 64, 0), ("KT", g * 64, 64, 64)], vc0=g * 64, dv=64,
                            qtiles=[dict(q0=t * 128, rows=128,
                                         maps=[dict(pb=0, dq=64, o0=(2 * t) * 64, head=2 * t),
                                               dict(pb=64, dq=64, o0=(2 * t + 1) * 64, head=2 * t + 1)])
                                    for t in range(4 * g, 4 * g + 4)])
                       for g in range(2)]
    return c


def chunks(n):
    out = []
    r = 0
    while r < n:
        m = min(512, n - r)
        out.append((r, m))
        r += m
    return out


def emit_layer(nc, P, L, sh):
    cfg = layer_cfg(L)
    NK, NQL = cfg["nk"], cfg["nq"]
    need_ctx = L < 3
    op = P.op

    def dram_in(name, shape, dt=F32):
        return P.dram("l%d_%s" % (L, name), shape, dt, kind="ExternalInput")

    XO, CX, GX = sh["XO"], sh["CX"], sh["GX"]
    if L == 0:
        xs = dram_in("xs", [NK, D])
        xq = dram_in("xq", [NQ, D])

    def ksrc(r0, n):
        if L == 0:
            return [(0, n // 128, xs, r0)]
        def grow(t):
            jr, w_ = t // TOK, t % TOK
            return ((w_ // 256) * 4 + jr) * 256 + (w_ % 256)
        if L in (1, 2):
            if r0 < SEQ:
                return [(2 * i, 2, GX[L - 1], grow(r0 + 256 * i)) for i in range(n // 256)]
            return [(0, n // 128, CX[L - 1], r0 - SEQ)]
        if r0 < TOK:
            return [(0, n // 128, XO[2], r0)]
        if r0 < TOK + 1024:
            res = []
            for j in range(n // 128):
                e_ = (r0 - TOK) // 128 + j
                res.append((j, 1, GX[2], grow((e_ // 2) * TOK + (0 if e_ % 2 == 0 else TOK - 128))))
            return res
        return [(0, n // 128, CX[2], r0 - TOK - 1024)]

    def qsrc(r0, n):
        if L == 0:
            return [(0, n // 128, xq, r0)]
        if r0 < TOK:
            return [(0, n // 128, XO[L - 1], r0)]
        return [(0, n // 128, CX[L - 1], r0 - TOK)]

    def odst(row):
        if L == 3:
            return sh["out"], row
        if row < TOK:
            return XO[L], row
        return CX[L], row - TOK
    cvec = dram_in("cvec", [128, 8, 2])
    ada_w = dram_in("ada_w", [D, 3 * D])
    ada_bf = dram_in("ada_bf", [128, 16])
    ada_bg = dram_in("ada_bg", [1, D])
    out_w = dram_in("out_w", [D, D])
    ln_g = dram_in("ln_g", [1, D])
    ln_b = dram_in("ln_b", [1, D])
    ident_d = dram_in("ident", [128, 128])
    tkc = dram_in("tkc", [128, NK])
    tks = dram_in("tks", [128, NK])
    tqc = dram_in("tqc", [128, NQ])
    tqs = dram_in("tqs", [128, NQ])
    if L == 0:
        wA = dram_in("wA", [D, 256 + 128 + 32 + 32])
        wG = dram_in("wG", [D, D])
        wqb = dram_in("wqb", [256, 2 * 1536])
        wkn = dram_in("wkn", [128, 1024])
        wv = dram_in("wv", [128, 1024])
        gq = dram_in("gq", [128, 2])
        gkv = dram_in("gkv", [128, 1])
    elif L == 1:
        wK = dram_in("wK", [D, 3072])
        wQ = dram_in("wQ", [D, 3072])
        lam = dram_in("lam", [1, 256])
        gsub = dram_in("gsub", [128, 1])
    elif L == 2:
        wK = dram_in("wK", [D, 768])
        wQ = dram_in("wQ", [D, 3072])
        gqk = dram_in("gqk", [128, 4])
    else:
        wK = dram_in("wK", [D, 384])
        wQ = dram_in("wQ", [D, 3072])
        sink = dram_in("sink", [1, 16])
        wmask = dram_in("wmask", [128, 14, 512])

    if L == 0:
        KN = P.dram("l%d_KN" % L, [1024, NK], BF16)
        KPE = P.dram("l%d_KPE" % L, [32, NK], BF16)
        kd = {"KN": KN, "KPE": KPE}
    else:
        KT = P.dram("l%d_KT" % L, [cfg["krows"], NK], BF16)
        kd = {"KT": KT}
    Vd = P.dram("l%d_Vd" % L, [NK, cfg["vcols"]], BF16)
    QT = P.dram("l%d_QT" % L, [cfg["qrows"], NQ], BF16)
    GT = P.dram("l%d_GT" % L, [D, NQ], BF16)
    OT = P.dram("l%d_OT" % L, [D, NQ], BF16)

    pb = sh["pb"]
    P.off = sh["sb0"]

    ident = P.sb("ident", [128, 128], F32)
    identb = P.sb("identb", [128, 128], BF16)
    onesb = P.sb("onesb", [128, 128], BF16)
    onesf = P.sb("onesf", [128, 128], F32)
    modf = P.sb("modf", [128, 16, 2], F32)
    s1p = P.sb("s1p", [128, 8, 2], F32)
    gate_bc = [P.sb("gate_bc%d" % j, [128, D], F32) for j in range(2)]
    small = P.sb("small", [128, 64], F32)
    epsc = P.sb("epsc", [128, 1], F32)
    persist_end = P.off

    def dma(eng, out_t, out_ap, in_t, in_ap, acc=False):
        return op(eng, lambda e: e.dma_start(out=out_ap, in_=in_ap), reads=[in_t], writes=[out_t], dma=out_t, acc=acc)

    dma("sync", ident, ident[:], ident_d, ident_d[:])
    op("vector", lambda e: e.tensor_copy(out=identb[:], in_=ident[:]), reads=[ident], writes=[identb])
    op("vector", lambda e: e.memset(onesb[:], 1.0), writes=[onesb])
    op("vector", lambda e: e.memset(onesf[:], 1.0), writes=[onesf])
    op("vector", lambda e: e.memset(epsc[:], EPS), writes=[epsc])

    cv = P.sb("cv", [128, 8, 2], F32)
    scv = P.sb("scv", [128, 8, 2], F32)
    scbc = P.sb("scbc", [128, 8, 2, 128], F32)
    abf = P.sb("abf", [128, 16], F32)
    abg = P.sb("abg", [128, D], F32)
    aw = [P.sb("aw%d" % i, [128, 8, 512], F32) for i in range(2)]
    dma("sync", cv, cv[:], cvec, cvec[:])
    dma("sync", abf, abf[:], ada_bf, ada_bf[:])
    dma("sync", abg, abg[:], ada_bg, ada_bg.ap()[0, :].partition_broadcast(128))
    op("scalar", lambda e: e.activation(out=scv[:], in_=cv[:], func=AF.Silu), reads=[cv], writes=[scv])
    for k in range(8):
        for j in range(2):
            op("vector", lambda e, k=k, j=j: e.tensor_scalar(out=scbc[:, k, j, :], in0=onesf[:], scalar1=scv[:, k, j:j + 1],
                                                           scalar2=None, op0=ALU.mult),
               reads=[onesf, scv], writes=[scbc], acc=True)
    aw_v = ada_w.ap().rearrange("(k p) n -> p k n", p=128)
    for pc in range(6):
        a = aw[pc % 2]
        dma("sync", a, a[:], ada_w, aw_v[:, :, pc * 512:(pc + 1) * 512])
        if pc < 4:
            for m in range(4):
                t = pc * 4 + m
                for k in range(8):
                    op("tensor", lambda e, a=a, m=m, k=k, t=t: e.matmul(pb[0][:, t * 2:t * 2 + 2], lhsT=a[:, k, m * 128:(m + 1) * 128],
                                                                     rhs=scv[:, k, :], start=(k == 0), stop=(k == 7)),
                       reads=[a, scv], writes=[pb[0]], acc=True)
        else:
            for j in range(2):
                for k in range(8):
                    op("tensor", lambda e, a=a, j=j, k=k: e.matmul(pb[1 + j][:, :], lhsT=scbc[:, k, j, :], rhs=a[:, k, :],
                                                                 start=(k == 0), stop=(k == 7)),
                       reads=[a, scbc], writes=[pb[1 + j]], acc=True)
                c0 = (pc - 4) * 512
                op("vector", lambda e, j=j, c0=c0: e.tensor_tensor(out=gate_bc[j][:, c0:c0 + 512], in0=pb[1 + j][:, :],
                                                                 in1=abg[:, c0:c0 + 512], op=ALU.add),
                   reads=[pb[1 + j], abg], writes=[gate_bc[j]], acc=True)
    for j in range(2):
        op("vector", lambda e, j=j: e.tensor_tensor(out=modf[:, :, j], in0=pb[0][:, j:32:2], in1=abf[:], op=ALU.add),
           reads=[pb[0], abf], writes=[modf], acc=True)
    op("vector", lambda e: e.tensor_scalar(out=s1p[:], in0=modf[:, 8:16, :], scalar1=1.0, scalar2=None, op0=ALU.add),
       reads=[modf], writes=[s1p])

    if L == 1:
        lam_init = 0.8 - 0.6 * math.exp(-0.3 * 1)
        lm = P.sb("lm", [1, 256], F32)
        lt = P.sb("lt", [1, 8], F32)
        gs = P.sb("gs", [128, 1], F32)
        dma("sync", lm, lm[:], lam, lam[:])
        dma("sync", gs, gs[:], gsub, gsub[:])
        op("vector", lambda e: e.tensor_tensor(out=lm[:, 0:64], in0=lm[:, 0:64], in1=lm[:, 64:128], op=ALU.mult), reads=[lm], writes=[lm])
        op("vector", lambda e: e.tensor_tensor(out=lm[:, 128:192], in0=lm[:, 128:192], in1=lm[:, 192:256], op=ALU.mult), reads=[lm], writes=[lm])
        op("vector", lambda e: e.tensor_reduce(out=lt[:, 0:1], in_=lm[:, 0:64], axis=mybir.AxisListType.X, op=ALU.add), reads=[lm], writes=[lt])
        op("vector", lambda e: e.tensor_reduce(out=lt[:, 1:2], in_=lm[:, 128:192], axis=mybir.AxisListType.X, op=ALU.add), reads=[lm], writes=[lt])
        op("scalar", lambda e: e.activation(out=lt[:, 2:4], in_=lt[:, 0:2], func=AF.Exp), reads=[lt], writes=[lt])
        op("vector", lambda e: e.tensor_tensor(out=lt[:, 4:5], in0=lt[:, 3:4], in1=lt[:, 2:3], op=ALU.subtract), reads=[lt], writes=[lt])
        op("vector", lambda e: e.tensor_scalar(out=lt[:, 4:5], in0=lt[:, 4:5], scalar1=-lam_init, scalar2=None, op0=ALU.add), reads=[lt], writes=[lt])
        op("tensor", lambda e: e.matmul(pb[3][:, 0:1], lhsT=onesf[0:1, :], rhs=lt[0:1, 4:5], start=True, stop=True),
           reads=[onesf, lt], writes=[pb[3]])
        op("vector", lambda e: e.tensor_copy(out=small[:, 0:1], in_=pb[3][:, 0:1]), reads=[pb[3]], writes=[small], acc=True)
        op("vector", lambda e: e.tensor_scalar(out=small[:, 1:2], in0=gs[:], scalar1=1.0 - lam_init, scalar2=None, op0=ALU.mult),
           reads=[gs], writes=[small], acc=True)
    if L == 3:
        sk = P.sb("sk", [128, 16], F32)
        dma("sync", sk, sk[:], sink, sink.ap()[0, :].partition_broadcast(128))
        op("scalar", lambda e: e.activation(out=small[:, 0:16], in_=sk[:], func=AF.Exp), reads=[sk], writes=[small], acc=True)
    if L == 0:
        gqs = P.sb("gqs", [128, 2], F32)
        gkvs = P.sb("gkvs", [128, 1], F32)
        dma("sync", gqs, gqs[:], gq, gq[:])
        dma("sync", gkvs, gkvs[:], gkv, gkv[:])
        op("vector", lambda e: e.tensor_copy(out=small[:, 0:2], in_=gqs[:]), reads=[gqs], writes=[small], acc=True)
        op("vector", lambda e: e.tensor_copy(out=small[:, 2:3], in_=gkvs[:]), reads=[gkvs], writes=[small], acc=True)
    if L == 2:
        gg = P.sb("gg", [128, 4], F32)
        dma("sync", gg, gg[:], gqk, gqk[:])
        op("vector", lambda e: e.tensor_copy(out=small[:, 0:4], in_=gg[:]), reads=[gg], writes=[small], acc=True)

    P.barrier()

    rr = {"cast": 0}

    def load_weights(dst, src, ncols, nk, stg):
        v = src.ap().rearrange("(k p) n -> p k n", p=128)
        step = 2048 // nk
        i = 0
        for c0 in range(0, ncols, step):
            w_ = min(step, ncols - c0)
            s = stg[i % 2]
            i += 1
            sv = s[:, 0:nk * w_].rearrange("p (k n) -> p k n", k=nk)
            dma("sync", s, sv, src, v[:, :, c0:c0 + w_])
            eng = "gpsimd" if i % 2 else "vector"
            op(eng, lambda e, sv=sv, c0=c0, w_=w_: e.tensor_copy(out=dst[:, :, c0:c0 + w_], in_=sv),
               reads=[s], writes=[dst], acc=True)

    def make_hT(src, row0, n, segs, bufs, it):
        xin = bufs["xin"][it % 2]
        xb = bufs["xb"][it % 2]
        hT = bufs["hT"][it % 2]
        nt = n // 128
        for (j0, ntl, st_, rw) in src(row0, n):
            dma("sync", xin, xin[:, j0:j0 + ntl, :], st_, st_.ap()[rw:rw + ntl * 128, :].rearrange("(j p) d -> p j d", p=128), acc=True)
        eng = "gpsimd" if (it % 2) else "vector"
        op(eng, lambda e: e.tensor_copy(out=xb[:, 0:nt, :], in_=xin[:, 0:nt, :]), reads=[xin], writes=[xb])
        for k in range(8):
            pt = pb[6 + (k % 2)]
            ptb = pt[:].bitcast(BF16)
            for j in range(nt):
                op("tensor", lambda e, k=k, j=j, ptb=ptb: e.transpose(out=ptb[:, j * 128:(j + 1) * 128],
                                                                     in_=xb[:, j, k * 128:(k + 1) * 128], identity=identb[:]),
                   reads=[xb, identb], writes=[pt], acc=True)
            for (c0, m, jj) in segs:
                op("scalar", lambda e, k=k, c0=c0, m=m, jj=jj, ptb=ptb: e.activation(
                    out=hT[:, k, c0:c0 + m], in_=ptb[:, c0:c0 + m], func=AF.Identity,
                    bias=modf[:, k, jj:jj + 1], scale=s1p[:, k, jj:jj + 1]),
                   reads=[pt, modf, s1p], writes=[hT], acc=True)
        return hT

    def proj(ps_t, rows, n, w_t, c0, src_t, src_fn, nk):
        for k in range(nk):
            op("tensor", lambda e, k=k: e.matmul(ps_t[0:rows, 0:n], lhsT=w_t[:, k, c0:c0 + rows], rhs=src_fn(k),
                                                 start=(k == 0), stop=(k == nk - 1)),
               reads=[w_t, src_t], writes=[ps_t], acc=True)

    def rope(out_t, out_ap, pa, pbk, rows, n, cos_t, cos_ap, sin_t, sin_ap, tmp):
        t1, t2 = tmp
        op("vector", lambda e: e.tensor_tensor(out=t1[0:rows, 0:n], in0=pa[0:rows, 0:n], in1=cos_ap, op=ALU.mult),
           reads=[pa, cos_t], writes=[t1])
        op("vector", lambda e: e.tensor_tensor(out=t2[0:rows, 0:n], in0=pbk[0:rows, 0:n], in1=sin_ap, op=ALU.mult),
           reads=[pbk, sin_t], writes=[t2])
        op("gpsimd", lambda e: e.tensor_tensor(out=out_ap, in0=t1[0:rows, 0:n], in1=t2[0:rows, 0:n], op=ALU.add),
           reads=[t1, t2], writes=[out_t])

    def rstd_bc(dst_t, ps_t, rows, n, inv_n):
        op("scalar", lambda e: e.activation(out=dst_t[0:rows, 0:n], in_=ps_t[0:rows, 0:n], func=AF.Sqrt, bias=epsc[0:rows, 0:1], scale=float(inv_n)),
           reads=[ps_t, epsc], writes=[dst_t])
        op("vector", lambda e: e.reciprocal(out=dst_t[0:rows, 0:n], in_=dst_t[0:rows, 0:n]), reads=[dst_t], writes=[dst_t])

    def phase_bufs():
        return dict(xin=[P.sb("xin%d" % i, [128, 4, D], F32) for i in range(2)],
                    xb=[P.sb("xb%d" % i, [128, 4, D], BF16) for i in range(2)],
                    hT=[P.sb("hT%d" % i, [128, 8, 512], BF16) for i in range(2)])

    def key_segs(r0, n):
        c_start = NK - CTX
        segs = []
        if r0 < c_start:
            m = min(n, c_start - r0)
            segs.append((0, m, 0))
            if m < n:
                segs.append((m, n - m, 1))
        else:
            segs.append((0, n, 1))
        return segs

    def q_segs(r0, n):
        return [(0, n, 0)] if r0 < TOK else [(0, n, 1)]

    P.off = persist_end
    bufs = phase_bufs()
    stg = [P.sb("stg%d" % i, [128, 2048], F32) for i in range(2)]
    tabc = [P.sb("tabc%d" % i, [128, 512], F32) for i in range(2)]
    tabs = [P.sb("tabs%d" % i, [128, 512], F32) for i in range(2)]
    tmp1 = [P.sb("tmpa%d" % i, [128, 512], F32) for i in range(2)]
    tmp2 = [P.sb("tmpb%d" % i, [128, 512], F32) for i in range(2)]
    kout = [P.sb("kout%d" % i, [128, 512], BF16) for i in range(3)]
    vout = [P.sb("vout%d" % i, [128, 1024], BF16) for i in range(2)]
    cnt = {"k": 0, "v": 0, "t": 0}

    def load_tab(tc_d, ts_d, r0, n, it, rows=128):
        a, b = tabc[it % 2], tabs[it % 2]
        dma("sync", a, a[0:rows, 0:n], tc_d, tc_d.ap()[0:rows, r0:r0 + n])
        dma("sync", b, b[0:rows, 0:n], ts_d, ts_d.ap()[0:rows, r0:r0 + n])
        return a, b

    def store_rows(dst_t, dst_ap, src_t, src_ap):
        dma("gpsimd", dst_t, dst_ap, src_t, src_ap, acc=True)

    def v_proj(src_t, src_fn, nk, w_t, c0, ncols, r0, n):
        for j in range(n // 128):
            vo = vout[cnt["v"] % 2]
            cnt["v"] += 1
            for c in range(0, ncols, 512):
                m = min(512, ncols - c)
                pv = pb[4 + (c // 512) % 2]
                for k in range(nk):
                    op("tensor", lambda e, k=k, j=j, c=c, m=m, pv=pv: e.matmul(pv[:, 0:m], lhsT=src_fn(k, j), rhs=w_t[:, k, c0 + c:c0 + c + m],
                                                                          start=(k == 0), stop=(k == nk - 1)),
                       reads=[w_t, src_t], writes=[pv], acc=True)
                op("scalar", lambda e, c=c, m=m, pv=pv, vo=vo: e.copy(out=vo[:, c:c + m], in_=pv[:, 0:m]),
                   reads=[pv], writes=[vo], acc=True)
            store_rows(Vd, Vd.ap()[r0 + j * 128:r0 + (j + 1) * 128, :], vo, vo[:, 0:ncols])

    if L in (1, 2, 3):
        nkc = {1: 1024, 2: 256, 3: 128}[L]
        wk_sb = P.sb("wk_sb", [128, 8, 3 * nkc], BF16)
        load_weights(wk_sb, wK, 3 * nkc, 8, stg)
        sq = P.sb("sq", [128, 512], F32)
        rs = P.sb("rs", [128, 512], F32)
        for it, (r0, n) in enumerate(chunks(NK)):
            hT = make_hT(ksrc, r0, n, key_segs(r0, n), bufs, it)
            tc_, ts_ = load_tab(tkc, tks, r0, n, it)
            for t in range(nkc // 128):
                pa, pbk = pb[0 + 2 * (t % 2)], pb[1 + 2 * (t % 2)]
                proj(pa, 128, n, wk_sb, t * 128, hT, lambda k, hT=hT, n=n: hT[:, k, 0:n], 8)
                proj(pbk, 128, n, wk_sb, nkc + t * 128, hT, lambda k, hT=hT, n=n: hT[:, k, 0:n], 8)
                ko = kout[cnt["k"] % 3]
                cnt["k"] += 1
                tm = (tmp1[cnt["t"] % 2], tmp2[cnt["t"] % 2])
                cnt["t"] += 1
                if L == 2:
                    op("scalar", lambda e, pa=pa, n=n: e.activation(out=sq[:, 0:n], in_=pa[:, 0:n], func=AF.Square), reads=[pa], writes=[sq])
                    op("tensor", lambda e, n=n: e.matmul(pb[4][:, 0:n], lhsT=onesf[:], rhs=sq[:, 0:n], start=True, stop=True),
                       reads=[onesf, sq], writes=[pb[4]])
                    rstd_bc(rs, pb[4], 128, n, 1.0 / 128)
                    op("vector", lambda e, tm=tm, pa=pa, tc_=tc_, n=n: e.scalar_tensor_tensor(
                        out=tm[0][:, 0:n], in0=pa[:, 0:n], scalar=small[:, 2:3], in1=tc_[:, 0:n], op0=ALU.mult, op1=ALU.mult),
                       reads=[pa, tc_, small], writes=[tm[0]])
                    op("vector", lambda e, tm=tm, pbk=pbk, ts_=ts_, n=n: e.scalar_tensor_tensor(
                        out=tm[1][:, 0:n], in0=pbk[:, 0:n], scalar=small[:, 3:4], in1=ts_[:, 0:n], op0=ALU.mult, op1=ALU.mult),
                       reads=[pbk, ts_, small], writes=[tm[1]])
                    op("gpsimd", lambda e, tm=tm, n=n: e.tensor_tensor(out=tm[0][:, 0:n], in0=tm[0][:, 0:n], in1=tm[1][:, 0:n], op=ALU.add),
                       reads=[tm[0], tm[1]], writes=[tm[0]])
                    op("vector", lambda e, tm=tm, ko=ko, n=n: e.tensor_tensor(out=ko[:, 0:n], in0=tm[0][:, 0:n], in1=rs[:, 0:n], op=ALU.mult),
                       reads=[tm[0], rs], writes=[ko])
                else:
                    rope(ko, ko[:, 0:n], pa, pbk, 128, n, tc_, tc_[:, 0:n], ts_, ts_[:, 0:n], tm)
                    pass
                store_rows(KT, KT.ap()[t * 128:(t + 1) * 128, r0:r0 + n], ko, ko[:, 0:n])
            v_proj(hT, lambda k, j, hT=hT: hT[:, k, j * 128:(j + 1) * 128], 8, wk_sb, 2 * nkc, nkc, r0, n)
    else:
        wa_sb = P.sb("wa_sb", [128, 8, 448], BF16)
        wkn_sb = P.sb("wkn_sb", [128, 1, 1024], BF16)
        wv_sb = P.sb("wv_sb", [128, 1, 1024], BF16)
        load_weights(wa_sb, wA, 448, 8, stg)
        load_weights(wkn_sb, wkn, 1024, 1, stg)
        load_weights(wv_sb, wv, 1024, 1, stg)
        sq = P.sb("sq", [128, 512], F32)
        rs = P.sb("rs", [128, 512], F32)
        kvn = [P.sb("kvn%d" % i, [128, 512], BF16) for i in range(2)]
        for it, (r0, n) in enumerate(chunks(NK)):
            hT = make_hT(ksrc, r0, n, key_segs(r0, n), bufs, it)
            tc_, ts_ = load_tab(tkc, tks, r0, n, it, rows=32)
            hsrc = lambda k, hT=hT, n=n: hT[:, k, 0:n]
            proj(pb[0], 128, n, wa_sb, 256, hT, hsrc, 8)
            op("scalar", lambda e, n=n: e.activation(out=sq[:, 0:n], in_=pb[0][:, 0:n], func=AF.Square), reads=[pb[0]], writes=[sq])
            op("tensor", lambda e, n=n: e.matmul(pb[4][:, 0:n], lhsT=onesf[:], rhs=sq[:, 0:n], start=True, stop=True),
               reads=[onesf, sq], writes=[pb[4]])
            rstd_bc(rs, pb[4], 128, n, 1.0 / 128)
            kv = kvn[it % 2]
            op("vector", lambda e, kv=kv, n=n: e.scalar_tensor_tensor(out=kv[:, 0:n], in0=pb[0][:, 0:n], scalar=small[:, 2:3], in1=rs[:, 0:n],
                                                                    op0=ALU.mult, op1=ALU.mult), reads=[pb[0], rs, small], writes=[kv])
            proj(pb[1], 32, n, wa_sb, 384, hT, hsrc, 8)
            proj(pb[2], 32, n, wa_sb, 416, hT, hsrc, 8)
            ko = kout[cnt["k"] % 3]
            cnt["k"] += 1
            tm = (tmp1[cnt["t"] % 2], tmp2[cnt["t"] % 2])
            cnt["t"] += 1
            rope(ko, ko[0:32, 0:n], pb[1], pb[2], 32, n, tc_, tc_[0:32, 0:n], ts_, ts_[0:32, 0:n], tm)
            store_rows(KPE, KPE.ap()[:, r0:r0 + n], ko, ko[0:32, 0:n])
            for t in range(8):
                pa = pb[2 + (t % 2)] if False else pb[3 if t % 2 else 1]
                proj(pa, 128, n, wkn_sb, t * 128, kv, lambda k, kv=kv, n=n: kv[:, 0:n], 1)
                ko = kout[cnt["k"] % 3]
                cnt["k"] += 1
                op("scalar", lambda e, ko=ko, pa=pa, n=n: e.copy(out=ko[:, 0:n], in_=pa[:, 0:n]), reads=[pa], writes=[ko])
                store_rows(KN, KN.ap()[t * 128:(t + 1) * 128, r0:r0 + n], ko, ko[:, 0:n])
            v_proj(kv, lambda k, j, kv=kv: kv[:, j * 128:(j + 1) * 128], 1, wv_sb, 0, 1024, r0, n)

    P.barrier()

    P.off = persist_end
    bufs = phase_bufs()
    stg = [P.sb("stg%d" % i, [128, 2048], F32) for i in range(2)]
    tabc = [P.sb("tabc%d" % i, [128, 512], F32) for i in range(2)]
    tabs = [P.sb("tabs%d" % i, [128, 512], F32) for i in range(2)]
    tmp1 = [P.sb("tmpa%d" % i, [128, 512], F32) for i in range(2)]
    tmp2 = [P.sb("tmpb%d" % i, [128, 512], F32) for i in range(2)]
    kout = [P.sb("kout%d" % i, [128, 512], BF16) for i in range(3)]
    sq = P.sb("sq", [128, 512], F32)
    rs = P.sb("rs", [128, 512], F32)
    cnt = {"k": 0, "v": 0, "t": 0}
    qchunks = chunks(NQL)

    def gate_proj(hT, n, w_t, c0, r0):
        for t in range(8):
            pa = pb[4 + (t % 2)]
            proj(pa, 128, n, w_t, c0 + t * 128, hT, lambda k, hT=hT, n=n: hT[:, k, 0:n], 8)
            ko = kout[cnt["k"] % 3]
            cnt["k"] += 1
            op("scalar", lambda e, ko=ko, pa=pa, n=n: e.activation(out=ko[:, 0:n], in_=pa[:, 0:n], func=AF.Silu), reads=[pa], writes=[ko])
            store_rows(GT, GT.ap()[t * 128:(t + 1) * 128, r0:r0 + n], ko, ko[:, 0:n])

    if L in (1, 2, 3):
        wq_sb = P.sb("wq_sb", [128, 8, 3072], BF16)
        load_weights(wq_sb, wQ, 3072, 8, stg)
        for it, (r0, n) in enumerate(qchunks):
            hT = make_hT(qsrc, r0, n, q_segs(r0, n), bufs, it)
            tc_, ts_ = load_tab(tqc, tqs, r0, n, it)
            for t in range(8):
                pa, pbk = pb[0 + 2 * (t % 2)], pb[1 + 2 * (t % 2)]
                proj(pa, 128, n, wq_sb, t * 128, hT, lambda k, hT=hT, n=n: hT[:, k, 0:n], 8)
                proj(pbk, 128, n, wq_sb, 1024 + t * 128, hT, lambda k, hT=hT, n=n: hT[:, k, 0:n], 8)
                ko = kout[cnt["k"] % 3]
                cnt["k"] += 1
                tm = (tmp1[cnt["t"] % 2], tmp2[cnt["t"] % 2])
                cnt["t"] += 1
                if L == 2:
                    op("scalar", lambda e, pa=pa, n=n: e.activation(out=sq[:, 0:n], in_=pa[:, 0:n], func=AF.Square), reads=[pa], writes=[sq])
                    op("tensor", lambda e, n=n: e.matmul(pb[6][:, 0:n], lhsT=onesf[:], rhs=sq[:, 0:n], start=True, stop=True),
                       reads=[onesf, sq], writes=[pb[6]])
                    rstd_bc(rs, pb[6], 128, n, 1.0 / 128)
                    op("vector", lambda e, tm=tm, pa=pa, tc_=tc_, n=n: e.scalar_tensor_tensor(
                        out=tm[0][:, 0:n], in0=pa[:, 0:n], scalar=small[:, 0:1], in1=tc_[:, 0:n], op0=ALU.mult, op1=ALU.mult),
                       reads=[pa, tc_, small], writes=[tm[0]])
                    op("vector", lambda e, tm=tm, pbk=pbk, ts_=ts_, n=n: e.scalar_tensor_tensor(
                        out=tm[1][:, 0:n], in0=pbk[:, 0:n], scalar=small[:, 1:2], in1=ts_[:, 0:n], op0=ALU.mult, op1=ALU.mult),
                       reads=[pbk, ts_, small], writes=[tm[1]])
                    op("gpsimd", lambda e, tm=tm, n=n: e.tensor_tensor(out=tm[0][:, 0:n], in0=tm[0][:, 0:n], in1=tm[1][:, 0:n], op=ALU.add),
                       reads=[tm[0], tm[1]], writes=[tm[0]])
                    op("vector", lambda e, tm=tm, ko=ko, n=n: e.tensor_tensor(out=ko[:, 0:n], in0=tm[0][:, 0:n], in1=rs[:, 0:n], op=ALU.mult),
                       reads=[tm[0], rs], writes=[ko])
                else:
                    rope(ko, ko[:, 0:n], pa, pbk, 128, n, tc_, tc_[:, 0:n], ts_, ts_[:, 0:n], tm)
                store_rows(QT, QT.ap()[t * 128:(t + 1) * 128, r0:r0 + n], ko, ko[:, 0:n])
            gate_proj(hT, n, wq_sb, 2048, r0)
    else:
        wa_sb = P.sb("wa_sb", [128, 8, 256], BF16)
        wg_sb = P.sb("wg_sb", [128, 8, 1024], BF16)
        wqb_sb = P.sb("wqb_sb", [128, 2, 3072], BF16)
        load_weights(wa_sb, wA, 256, 8, stg)
        load_weights(wg_sb, wG, 1024, 8, stg)
        load_weights(wqb_sb, wqb, 3072, 2, stg)
        sq2 = P.sb("sq2", [128, 2, 512], F32)
        qn = [P.sb("qn%d" % i, [128, 2, 512], BF16) for i in range(2)]
        for it, (r0, n) in enumerate(qchunks):
            hT = make_hT(qsrc, r0, n, q_segs(r0, n), bufs, it)
            tc_, ts_ = load_tab(tqc, tqs, r0, n, it, rows=96)
            hsrc = lambda k, hT=hT, n=n: hT[:, k, 0:n]
            proj(pb[0], 128, n, wa_sb, 0, hT, hsrc, 8)
            proj(pb[1], 128, n, wa_sb, 128, hT, hsrc, 8)
            for u in range(2):
                op("scalar", lambda e, u=u, n=n: e.activation(out=sq2[:, u, 0:n], in_=pb[u][:, 0:n], func=AF.Square),
                   reads=[pb[u]], writes=[sq2], acc=True)
            for u in range(2):
                op("tensor", lambda e, u=u, n=n: e.matmul(pb[4][:, 0:n], lhsT=onesf[:], rhs=sq2[:, u, 0:n], start=(u == 0), stop=(u == 1)),
                   reads=[onesf, sq2], writes=[pb[4]], acc=True)
            rstd_bc(rs, pb[4], 128, n, 1.0 / 256)
            q_ = qn[it % 2]
            for u in range(2):
                op("vector", lambda e, u=u, q_=q_, n=n: e.scalar_tensor_tensor(out=q_[:, u, 0:n], in0=pb[u][:, 0:n], scalar=small[:, u:u + 1],
                                                                             in1=rs[:, 0:n], op0=ALU.mult, op1=ALU.mult),
                   reads=[pb[u], rs, small], writes=[q_], acc=True)
            for h in range(16):
                pa, pbk = pb[0 + 2 * (h % 2)], pb[1 + 2 * (h % 2)]
                proj(pa, 96, n, wqb_sb, h * 96, q_, lambda k, q_=q_, n=n: q_[:, k, 0:n], 2)
                proj(pbk, 96, n, wqb_sb, 1536 + h * 96, q_, lambda k, q_=q_, n=n: q_[:, k, 0:n], 2)
                ko = kout[cnt["k"] % 3]
                cnt["k"] += 1
                tm = (tmp1[cnt["t"] % 2], tmp2[cnt["t"] % 2])
                cnt["t"] += 1
                rope(ko, ko[0:96, 0:n], pa, pbk, 96, n, tc_, tc_[0:96, 0:n], ts_, ts_[0:96, 0:n], tm)
                store_rows(QT, QT.ap()[h * 96:(h + 1) * 96, r0:r0 + n], ko, ko[0:96, 0:n])
            gate_proj(hT, n, wg_sb, 0, r0)

    P.barrier()

    P.off = persist_end
    NT = NK // 128
    ktb = [P.sb("ktb%d" % i, [128, NK], BF16) for i in range(2)]
    vtb = [P.sb("vtb%d" % i, [128, NT, 128], BF16) for i in range(2)]
    nqt = 2 if L == 1 else 1
    qtb = [[P.sb("qtb%d_%d" % (i, j), [128, NQ], BF16) for j in range(nqt)] for i in range(2)]
    if L == 1:
        for qq in qtb:
            for q_ in qq:
                op("gpsimd", lambda e, q_=q_: e.memset(q_[:], 0.0), writes=[q_])
    pT = [P.sb("pT%d" % i, [128, 512], BF16) for i in range(4)]
    fin_a = [P.sb("fin_a%d" % i, [128, 512], F32) for i in range(2)]
    fin_b = [P.sb("fin_b%d" % i, [128, 512], F32) for i in range(2)]
    fin_c = P.sb("fin_c", [128, 512], F32)
    fin_d = P.sb("fin_d", [128, 512], F32)
    oout = [P.sb("oout%d" % i, [128, 512], BF16) for i in range(2)]
    lacc = [(P.sb("laD%d" % i, [128, 512], F32), P.sb("laP%d" % i, [128, 512], F32)) for i in range(2)] if cfg["fin"] in ("diff", "dv128") else [(None, None)] * 2
    if L == 3:
        msk = P.sb("msk", [128, 14, 512], BF16)
        mskf = P.sb("mskf", [128, 14, 512], F32)
        dma("sync", mskf, mskf[:], wmask, wmask[:])
        op("vector", lambda e: e.tensor_copy(out=msk[:], in_=mskf[:]), reads=[mskf], writes=[msk])
    dv = cfg["groups"][0]["dv"]
    if dv == 64:
        for v_ in vtb:
            op("vector", lambda e, v_=v_: e.memset(v_[:, :, 64:128], 1.0), writes=[v_])
    sc_att = float(cfg["scale"])
    Sps = [pb[0], pb[1], pb[2]]
    acc = [pb[3], pb[4], pb[5], pb[6]]
    st = {"s": 0, "p": 0, "a": 0, "f": 0, "o": 0, "q": 0, "l": 0}

    def key_tiles(ci, r0):
        if L < 3:
            if r0 >= TOK:
                return [(NT - 2, None), (NT - 1, None)]
            return [(t, None) for t in range(NT)]
        res = []
        for rel in range(-1, 5):
            kt = 4 * ci + rel
            if 0 <= kt < 32:
                res.append((kt, rel + 1))
        if ci == 0:
            res += [(32 + e_, 6 + e_) for e_ in (1, 3, 5, 7)]
        if ci == 7:
            res += [(32 + e_, 6 + e_) for e_ in (0, 2, 4, 6)]
        res += [(40, None), (41, None)]
        return res

    for gi, g in enumerate(cfg["groups"]):
        kt_sb = ktb[gi % 2]
        v_sb = vtb[gi % 2]
        for (nm, r0k, rows, p0) in g["kparts"]:
            src = kd[nm]
            dma("sync", kt_sb, kt_sb[p0:p0 + rows, :], src, src.ap()[r0k:r0k + rows, :], acc=True)
        dvv = g["dv"]
        vsrc = Vd.ap()[:, g["vc0"]:g["vc0"] + dvv].rearrange("(t p) c -> p t c", p=128)
        nsp = 4 if NT >= 64 else 2
        per = (NT + nsp - 1) // nsp
        for s_ in range(nsp):
            t0, t1 = s_ * per, min(NT, (s_ + 1) * per)
            dma("sync", v_sb, v_sb[:, t0:t1, 0:dvv], Vd, vsrc[:, t0:t1, :], acc=True)
        for qt in g["qtiles"]:
            q_sbs = qtb[st["q"] % 2]
            st["q"] += 1
            loads = qt.get("loads") or [(0, qt["q0"], qt["rows"], 0)]
            for (qi_, q0_, rows_, p0_) in loads:
                dma("sync", q_sbs[qi_], q_sbs[qi_][p0_:p0_ + rows_, 0:NQL], QT, QT.ap()[q0_:q0_ + rows_, 0:NQL], acc=True)
            for ci, (r0, n) in enumerate(qchunks):
                tiles = key_tiles(ci, r0)
                maps = qt["maps"]
                accs = []
                for mi, mp in enumerate(maps):
                    if cfg["fin"] == "diff":
                        a_o, a_l = acc[2 * mi], acc[2 * mi + 1]
                    elif cfg["fin"] == "dv128":
                        a_o, a_l = acc[2 * (st["a"] % 2)], acc[2 * (st["a"] % 2) + 1]
                        st["a"] += 1
                    else:
                        a_o = acc[st["a"] % 4]
                        a_l = None
                        st["a"] += 1
                    accs.append((a_o, a_l))
                    pb_, dq = mp["pb"], mp["dq"]
                    q_sb = q_sbs[mp.get("qi", 0)]
                    nt_ = len(tiles)
                    pend = []

                    def issue_s(ti):
                        kt, mk = tiles[ti]
                        sp = Sps[st["s"] % 3]
                        st["s"] += 1
                        op("tensor", lambda e, kt=kt, sp=sp: e.matmul(sp[:, 0:n], lhsT=kt_sb[pb_:pb_ + dq, kt * 128:(kt + 1) * 128],
                                                                     rhs=q_sb[pb_:pb_ + dq, r0:r0 + n], start=True, stop=True),
                           reads=[kt_sb, q_sb], writes=[sp])
                        pt_ = pT[st["p"] % 4]
                        st["p"] += 1
                        op("scalar", lambda e, sp=sp, pt_=pt_: e.activation(out=pt_[:, 0:n], in_=sp[:, 0:n], func=AF.Exp, scale=sc_att),
                           reads=[sp], writes=[pt_])
                        if mk is not None:
                            op("vector", lambda e, pt_=pt_, mk=mk: e.tensor_tensor(out=pt_[:, 0:n], in0=pt_[:, 0:n], in1=msk[:, mk, 0:n], op=ALU.mult),
                               reads=[pt_, msk], writes=[pt_])
                        return (kt, pt_)

                    def issue_pv(ti, kt, pt_):
                        first, last = (ti == 0), (ti == nt_ - 1)
                        op("tensor", lambda e: e.matmul(a_o[:, 0:n], lhsT=v_sb[:, kt, :], rhs=pt_[:, 0:n], start=first, stop=last),
                           reads=[v_sb, pt_], writes=[a_o], acc=not first)
                        if a_l is not None:
                            if ti % 3 == 2:
                                op("tensor", lambda e, fs=(not used["pe"]): e.matmul(a_l[:, 0:n], lhsT=onesb[:], rhs=pt_[:, 0:n], start=fs, stop=False),
                                   reads=[onesb, pt_], writes=[a_l], acc=used["pe"])
                                used["pe"] = True
                            elif not used["vector"]:
                                used["vector"] = True
                                op("vector", lambda e: e.tensor_copy(out=laD[:, 0:n], in_=pt_[:, 0:n]), reads=[pt_], writes=[laD])
                            else:
                                op("vector", lambda e: e.tensor_tensor(out=laD[:, 0:n], in0=laD[:, 0:n], in1=pt_[:, 0:n], op=ALU.add),
                                   reads=[pt_, laD], writes=[laD])

                    used = {"vector": False, "pe": False}
                    laD, laP = lacc[st["l"] % 2]
                    st["l"] += 1
                    LOOK = 2
                    for ti in range(nt_ + LOOK):
                        if ti < nt_:
                            pend.append(issue_s(ti))
                        if ti >= LOOK:
                            kt, pt_ = pend[ti - LOOK]
                            issue_pv(ti - LOOK, kt, pt_)
                    if a_l is not None:
                        op("tensor", lambda e, fs=(not used["pe"]): e.matmul(a_l[:, 0:n], lhsT=onesf[:], rhs=laD[:, 0:n], start=fs, stop=True),
                           reads=[onesf, laD], writes=[a_l], acc=used["pe"])

                fa, fb = fin_a[st["f"] % 2], fin_b[st["f"] % 2]
                st["f"] += 1
                if cfg["fin"] == "dv128":
                    (a_o, a_l) = accs[0]
                    oo = oout[st["o"] % 2]
                    st["o"] += 1
                    op("vector", lambda e, a_l=a_l, fa=fa: e.reciprocal(out=fa[:, 0:n], in_=a_l[:, 0:n]), reads=[a_l], writes=[fa])
                    op("vector", lambda e, a_o=a_o, fa=fa, oo=oo: e.tensor_tensor(out=oo[:, 0:n], in0=a_o[:, 0:n], in1=fa[:, 0:n], op=ALU.mult),
                       reads=[a_o, fa], writes=[oo])
                    o0 = maps[0]["o0"]
                    store_rows(OT, OT.ap()[o0:o0 + 128, r0:r0 + n], oo, oo[:, 0:n])
                elif cfg["fin"] == "dv64":
                    for mi, mp in enumerate(maps):
                        (a_o, _) = accs[mi]
                        oo = oout[st["o"] % 2]
                        st["o"] += 1
                        fa = fin_a[st["f"] % 2]
                        st["f"] += 1
                        if L == 3:
                            hcol = mp["head"]
                            op("vector", lambda e, a_o=a_o, fa=fa, hcol=hcol: e.tensor_scalar(out=fa[64:128, 0:n], in0=a_o[64:128, 0:n],
                                                                                          scalar1=small[64:128, hcol:hcol + 1], scalar2=None, op0=ALU.add),
                               reads=[a_o, small], writes=[fa])
                            op("vector", lambda e, fa=fa: e.reciprocal(out=fa[64:128, 0:n], in_=fa[64:128, 0:n]), reads=[fa], writes=[fa])
                        else:
                            op("vector", lambda e, a_o=a_o, fa=fa: e.reciprocal(out=fa[64:128, 0:n], in_=a_o[64:128, 0:n]), reads=[a_o], writes=[fa])
                        op("vector", lambda e, a_o=a_o, fa=fa, oo=oo: e.tensor_tensor(out=oo[0:64, 0:n], in0=a_o[0:64, 0:n], in1=fa[64:128, 0:n], op=ALU.mult),
                           reads=[a_o, fa], writes=[oo])
                        o0 = mp["o0"]
                        store_rows(OT, OT.ap()[o0:o0 + 64, r0:r0 + n], oo, oo[0:64, 0:n])
                else:
                    (o1, l1), (o2, l2) = accs
                    oo = oout[st["o"] % 2]
                    st["o"] += 1
                    op("vector", lambda e: e.reciprocal(out=fa[:, 0:n], in_=l1[:, 0:n]), reads=[l1], writes=[fa])
                    op("vector", lambda e: e.tensor_tensor(out=fa[:, 0:n], in0=o1[:, 0:n], in1=fa[:, 0:n], op=ALU.mult), reads=[o1, fa], writes=[fa])
                    op("vector", lambda e: e.reciprocal(out=fb[:, 0:n], in_=l2[:, 0:n]), reads=[l2], writes=[fb])
                    op("vector", lambda e: e.tensor_tensor(out=fb[:, 0:n], in0=o2[:, 0:n], in1=fb[:, 0:n], op=ALU.mult), reads=[o2, fb], writes=[fb])
                    op("vector", lambda e: e.scalar_tensor_tensor(out=fa[:, 0:n], in0=fb[:, 0:n], scalar=small[:, 0:1], in1=fa[:, 0:n],
                                                                op0=ALU.mult, op1=ALU.add), reads=[fa, fb, small], writes=[fa])
                    op("gpsimd", lambda e: e.tensor_tensor(out=fin_c[:, 0:n], in0=fa[:, 0:n], in1=fa[:, 0:n], op=ALU.mult), reads=[fa], writes=[fin_c])
                    op("tensor", lambda e: e.matmul(pb[7][:, 0:n], lhsT=onesf[:], rhs=fin_c[:, 0:n], start=True, stop=True),
                       reads=[onesf, fin_c], writes=[pb[7]])
                    rstd_bc(fin_d, pb[7], 128, n, 1.0 / 128)
                    op("vector", lambda e: e.scalar_tensor_tensor(out=oo[:, 0:n], in0=fa[:, 0:n], scalar=small[:, 1:2], in1=fin_d[:, 0:n],
                                                                op0=ALU.mult, op1=ALU.mult), reads=[fa, fin_d, small], writes=[oo])
                    o0 = maps[0]["o0"]
                    store_rows(OT, OT.ap()[o0:o0 + 128, r0:r0 + n], oo, oo[:, 0:n])

    P.barrier()

    P.off = persist_end
    stg = [P.sb("stg%d" % i, [128, 2048], F32) for i in range(2)]
    wo_sb = P.sb("wo_sb", [128, 8, D], BF16)
    load_weights(wo_sb, out_w, D, 8, stg)
    lng = P.sb("lng", [128, D], F32)
    lnb = P.sb("lnb", [128, D], F32)
    dma("sync", lng, lng[:], ln_g, ln_g.ap()[0, :].partition_broadcast(128))
    dma("sync", lnb, lnb[:], ln_b, ln_b.ap()[0, :].partition_broadcast(128))
    otb = [P.sb("otb%d" % i, [128, 8, 512], BF16) for i in range(2)]
    gtb = [P.sb("gtb%d" % i, [128, 8, 512], BF16) for i in range(2)]
    ogb = [P.sb("ogb%d" % i, [128, 8, 512], BF16) for i in range(2)]
    xres = [P.sb("xres%d" % i, [128, D], F32) for i in range(2)]
    yt = [P.sb("yt%d" % i, [128, D], F32) for i in range(2)]
    stt = [P.sb("stt%d" % i, [128, 16], F32) for i in range(2)]
    ti_ = 0
    for it, (r0, n) in enumerate(qchunks):
        ot, gt, og = otb[it % 2], gtb[it % 2], ogb[it % 2]
        dma("sync", ot, ot[:, :, 0:n], OT, OT.ap()[:, r0:r0 + n].rearrange("(k p) t -> p k t", p=128))
        dma("sync", gt, gt[:, :, 0:n], GT, GT.ap()[:, r0:r0 + n].rearrange("(k p) t -> p k t", p=128))
        op("gpsimd", lambda e, ot=ot, gt=gt, og=og, n=n: e.tensor_tensor(out=og[:, :, 0:n], in0=ot[:, :, 0:n], in1=gt[:, :, 0:n], op=ALU.mult),
           reads=[ot, gt], writes=[og])
        jj = 0 if r0 < TOK else 1
        for j in range(n // 128):
            xr, y, sv = xres[ti_ % 2], yt[ti_ % 2], stt[ti_ % 2]
            pa, pbk = pb[2 * (ti_ % 2)], pb[2 * (ti_ % 2) + 1]
            ti_ += 1
            row = r0 + j * 128
            (_, _, xsrc_t, xsrc_r) = qsrc(row, 128)[0]
            dma("sync", xr, xr[:], xsrc_t, xsrc_t.ap()[xsrc_r:xsrc_r + 128, :])
            for hf, pp in enumerate((pa, pbk)):
                for k in range(8):
                    op("tensor", lambda e, k=k, j=j, hf=hf, pp=pp, og=og: e.matmul(pp[:, :], lhsT=og[:, k, j * 128:(j + 1) * 128],
                                                                               rhs=wo_sb[:, k, hf * 512:(hf + 1) * 512], start=(k == 0), stop=(k == 7)),
                       reads=[og, wo_sb], writes=[pp], acc=True)
            for hf, pp in enumerate((pa, pbk)):
                op("vector", lambda e, hf=hf, pp=pp, y=y, jj=jj: e.tensor_tensor(out=y[:, hf * 512:(hf + 1) * 512], in0=pp[:, :],
                                                                             in1=gate_bc[jj][:, hf * 512:(hf + 1) * 512], op=ALU.mult),
                   reads=[pp, gate_bc[jj]], writes=[y], acc=True)
            op("vector", lambda e, y=y, xr=xr: e.scalar_tensor_tensor(out=y[:], in0=xr[:], scalar=float(ALPHA), in1=y[:], op0=ALU.mult, op1=ALU.add),
               reads=[xr, y], writes=[y])
            for hf in range(2):
                op("vector", lambda e, hf=hf, y=y, sv=sv: e.bn_stats(out=sv[:, hf * 6:(hf + 1) * 6], in_=y[:, hf * 512:(hf + 1) * 512]),
                   reads=[y], writes=[sv], acc=True)
            op("vector", lambda e, sv=sv: e.bn_aggr(out=sv[:, 12:14], in_=sv[:, 0:12]), reads=[sv], writes=[sv])
            op("scalar", lambda e, sv=sv: e.activation(out=sv[:, 14:15], in_=sv[:, 13:14], func=AF.Sqrt, bias=epsc[:, 0:1], scale=1.0),
               reads=[sv, epsc], writes=[sv])
            op("vector", lambda e, sv=sv: e.reciprocal(out=sv[:, 14:15], in_=sv[:, 14:15]), reads=[sv], writes=[sv])
            op("vector", lambda e, y=y, sv=sv: e.tensor_scalar(out=y[:], in0=y[:], scalar1=sv[:, 12:13], scalar2=sv[:, 14:15],
                                                             op0=ALU.subtract, op1=ALU.mult), reads=[y, sv], writes=[y])
            op("gpsimd", lambda e, y=y: e.tensor_tensor(out=y[:], in0=y[:], in1=lng[:], op=ALU.mult), reads=[y, lng], writes=[y])
            op("gpsimd", lambda e, y=y: e.tensor_tensor(out=y[:], in0=y[:], in1=lnb[:], op=ALU.add), reads=[y, lnb], writes=[y])
            od_t, od_r = odst(row)
            dma("gpsimd", od_t, od_t.ap()[od_r:od_r + 128, :], y, y[:], acc=True)
    if L < 3:
        for c_ in range(TOK // 256):
            op("gpsimd", lambda e, c_=c_: e.collective_compute(
                "AllGather", ALU.bypass, replica_groups=[[0, 1, 2, 3], [4, 5, 6, 7]],
                ins=[XO[L].ap()[c_ * 256:(c_ + 1) * 256, :].opt()], outs=[GX[L].ap()[c_ * 1024:(c_ + 1) * 1024, :].opt()]),
               reads=[XO[L]], writes=[GX[L]], dma=GX[L], dma_inc=1, acc=True)
    else:
        op("gpsimd", lambda e: e.nop(), reads=[sh["out"]])
    P.barrier()


def build_all():
    nc = bass.Bass("TRN2", target_bir_lowering=False)
    P = Prog(nc)
    sh = dict(pb=[P.ps("pb%d" % i, [128, 512], F32) for i in range(8)], sb0=P.off)
    sh["XO"] = [P.dram("XO%d" % i, [TOK, D], F32) for i in range(3)]
    sh["CX"] = [P.dram("CX%d" % i, [CTX, D], F32) for i in range(3)]
    sh["GX"] = [P.dram("GX%d" % i, [SEQ, D], F32) for i in range(3)]
    sh["out"] = P.dram("out", [TOK, D], F32, kind="ExternalOutput")
    for L in range(4):
        emit_layer(nc, P, L, sh)
    P.emit()
    return nc


def rope_table(positions, rot_dim):
    pos = np.asarray(positions)
    row = (pos // GRID_W).astype(np.float32)
    col = (pos % GRID_W).astype(np.float32)
    n_freq = rot_dim // 4
    inv = (np.float32(THETA) ** (-np.arange(n_freq, dtype=np.float32) / np.float32(n_freq))).astype(np.float32)
    ang = np.concatenate([row[:, None] * inv[None, :], col[:, None] * inv[None, :]], axis=-1).astype(np.float32)
    cos = np.cos(ang).astype(np.float32).T
    sin = np.sin(ang).astype(np.float32).T
    c = np.concatenate([cos, cos], axis=0)
    s = np.concatenate([-sin, sin], axis=0)
    return c, s


def swap_halves(w, head_dim):
    n = w.shape[-1]
    idx = np.arange(n).reshape(-1, head_dim)
    half = head_dim // 2
    idx = np.concatenate([idx[:, half:], idx[:, :half]], axis=1).reshape(-1)
    return w[..., idx]


def tile_tab(c, s, reps, ntok_ctx):
    c = np.concatenate([np.tile(c, (reps, 1)), np.ones((c.shape[0] * reps, ntok_ctx), np.float32)], axis=1)
    s = np.concatenate([np.tile(s, (reps, 1)), np.zeros((s.shape[0] * reps, ntok_ctx), np.float32)], axis=1)
    return c, s


def pad_rows(a, rows=128, fill=0.0):
    if a.shape[0] == rows:
        return np.ascontiguousarray(a)
    out = np.full((rows, a.shape[1]), fill, np.float32)
    out[:a.shape[0]] = a
    return out


_PROG = None


def get_prog():
    global _PROG
    if _PROG is None:
        _PROG = build_all()
    return _PROG


def fm(v):
    return np.ascontiguousarray(np.asarray(v, np.float32).reshape(8, 128).T)


def layer_inputs(L, inp):
    maps = []
    f32 = np.float32
    ada_w = np.ascontiguousarray(inp["ada_w"][L])
    ada_b = inp["ada_b"][L]
    ada_bf = np.concatenate([fm(ada_b[0:1024]), fm(ada_b[1024:2048])], axis=1)
    ada_bg = np.ascontiguousarray(ada_b[2048:3072][None, :])
    common = dict(ada_w=ada_w, ada_bf=np.ascontiguousarray(ada_bf), ada_bg=ada_bg,
                  out_w=np.ascontiguousarray(inp["out_w"][L]), ln_g=np.ascontiguousarray(inp["ln_g"][L][None, :]),
                  ln_b=np.ascontiguousarray(inp["ln_b"][L][None, :]), ident=np.eye(128, dtype=f32))
    if L == 0:
        w_in = inp["mla_w_in"][0]
        kpe = w_in[:, 384:416]
        common["wA"] = np.ascontiguousarray(np.concatenate([w_in[:, 0:384], kpe, swap_halves(kpe, 32)], axis=1))
        common["wG"] = np.ascontiguousarray(w_in[:, 416:1440])
        wqb = inp["mla_w_qb"][0].reshape(256, 16, 96)
        wqb_sw = np.concatenate([wqb[:, :, 0:64], swap_halves(wqb[:, :, 64:96], 32)], axis=2)
        common["wqb"] = np.ascontiguousarray(np.concatenate([wqb.reshape(256, 1536), wqb_sw.reshape(256, 1536)], axis=1))
        wkvb = inp["mla_w_kvb"][0].reshape(128, 16, 128)
        common["wkn"] = np.ascontiguousarray(wkvb[:, :, 0:64].reshape(128, 1024))
        common["wv"] = np.ascontiguousarray(wkvb[:, :, 64:128].reshape(128, 1024))
        common["gq"] = np.ascontiguousarray(inp["mla_g_qa"][0].reshape(2, 128).T)
        common["gkv"] = np.ascontiguousarray(inp["mla_g_kva"][0].reshape(1, 128).T)
    elif L == 1:
        w_in = inp["diff_w_in"][0]
        q, k, v, g = w_in[:, 0:1024], w_in[:, 1024:2048], w_in[:, 2048:3072], w_in[:, 3072:4096]
        common["wK"] = np.ascontiguousarray(np.concatenate([k, swap_halves(k, 64), v], axis=1))
        common["wQ"] = np.ascontiguousarray(np.concatenate([q, swap_halves(q, 64), g], axis=1))
        common["lam"] = np.ascontiguousarray(inp["diff_lambda"][0].reshape(1, 256))
        common["gsub"] = np.ascontiguousarray(inp["diff_g_sub"][0].reshape(128, 1))
    elif L == 2:
        w_in = inp["gqa_w_in"][0]
        q, k, v, g = w_in[:, 0:1024], w_in[:, 1024:1280], w_in[:, 1280:1536], w_in[:, 1536:2560]
        common["wK"] = np.ascontiguousarray(np.concatenate([k, swap_halves(k, 128), v], axis=1))
        common["wQ"] = np.ascontiguousarray(np.concatenate([q, swap_halves(q, 128), g], axis=1))
        gq_, gk_ = inp["gqa_g_q"][0], inp["gqa_g_k"][0]
        common["gqk"] = np.ascontiguousarray(np.stack([gq_, swap_halves(gq_[None], 128)[0], gk_, swap_halves(gk_[None], 128)[0]], axis=1))
    else:
        w_in = inp["swa_w_in"][0]
        q, k, v, g = w_in[:, 0:1024], w_in[:, 1024:1152], w_in[:, 1152:1280], w_in[:, 1280:2304]
        common["wK"] = np.ascontiguousarray(np.concatenate([k, swap_halves(k, 64), v], axis=1))
        common["wQ"] = np.ascontiguousarray(np.concatenate([q, swap_halves(q, 64), g], axis=1))
        common["sink"] = np.ascontiguousarray(inp["swa_sink"][0].reshape(1, 16))
    rot = {0: 32, 1: 64, 2: 128, 3: 64}[L]
    reps = {0: 1, 1: 2, 2: 1, 3: 2}[L]
    for r in range(NCORE):
        b, qr = r // 4, r % 4
        t0 = qr * TOK
        m = dict(common)
        cv = np.stack([fm(inp["c"][b]), fm(inp["c_ctx"])], axis=2)
        m["cvec"] = np.ascontiguousarray(cv)
        qpos = np.arange(t0, t0 + TOK)
        if L == 0:
            own = inp["x"][b, t0:t0 + TOK]
            m["xq"] = np.ascontiguousarray(np.concatenate([own, inp["ctx"][b]], axis=0))
            m["xs"] = np.ascontiguousarray(np.concatenate([inp["x"][b], inp["ctx"][b]], axis=0))
        if L < 3:
            kpos = np.arange(SEQ)
        else:
            edge = []
            for e_ in range(8):
                base = (e_ // 2) * TOK + (0 if e_ % 2 == 0 else TOK - 128)
                edge.append(np.arange(base, base + 128))
            kpos = np.concatenate([qpos] + edge)
            jl = np.arange(128)[:, None]
            il = np.arange(128)[None, :]
            lo = (jl >= il).astype(f32)
            up = (jl <= il).astype(f32)
            full = np.ones((128, 128), f32)
            zero = np.zeros((128, 128), f32)
            wm = np.zeros((128, 14, 512), f32)
            for rel in range(-1, 5):
                blocks = []
                for a in range(4):
                    d_ = rel - a
                    blocks.append(lo if d_ == -1 else full if d_ == 0 else up if d_ == 1 else zero)
                wm[:, rel + 1, :] = np.concatenate(blocks, axis=1)
            if qr > 0:
                wm[:, 6 + 2 * (qr - 1) + 1, :] = wm[:, 0, :]
            if qr < 3:
                wm[:, 6 + 2 * (qr + 1), :] = wm[:, 5, :]
            m["wmask"] = wm
        ck, sk = rope_table(kpos, rot)
        cq, sq_ = rope_table(qpos, rot)
        if L == 0:
            cqf = np.concatenate([np.ones((64, TOK), f32), cq], axis=0)
            sqf = np.concatenate([np.zeros((64, TOK), f32), sq_], axis=0)
            cq, sq_ = tile_tab(cqf, sqf, 1, CTX)
            ck, sk = tile_tab(ck, sk, 1, CTX)
        else:
            cq, sq_ = tile_tab(cq, sq_, reps, CTX)
            ck, sk = tile_tab(ck, sk, reps, CTX)
        m["tkc"], m["tks"] = pad_rows(ck), pad_rows(sk)
        m["tqc"], m["tqs"] = pad_rows(cq), pad_rows(sq_)
        maps.append(m)
    return maps


def kernel(**inputs):
    inp = {k: np.ascontiguousarray(np.asarray(v), dtype=np.float32) for k, v in inputs.items()}
    nc = get_prog()
    in_maps = [dict() for _ in range(NCORE)]
    for L in range(4):
        lm = layer_inputs(L, inp)
        for r in range(NCORE):
            for k, v in lm[r].items():
                in_maps[r]["l%d_%s" % (L, k)] = v
    res = run_bass_kernel_spmd(nc, in_maps, core_ids=list(range(NCORE)))
    out = np.empty((2, SEQ, D), np.float32)
    for r in range(NCORE):
        b, qr = r // 4, r % 4
        out[b, qr * TOK:(qr + 1) * TOK] = np.asarray(res.results[r]["out"]).reshape(TOK, D)
    return out
```
